# Optimizing a Trainium2 kernel written in Bass

```python
import math
import jax
import jax.numpy as jnp
from jax import lax
import numpy as np

D_MODEL = 2048
BATCH = 8
SEQ = 2048
DEPTH = 2

GRID_W = 64
CTX_LEN = 256
N_EVEN = (DEPTH + 1) // 2
N_ODD = DEPTH // 2
N_MOD = 6
D_FF = 4 * D_MODEL
NORM_EPS = 1e-6
ROPE_BASE = 10000.0
NEG_INF = -1e30

MLA_HEADS = 8
MLA_Q_LORA = 512
MLA_KV_LORA = 512
MLA_NOPE = 128
MLA_ROPE = 64
MLA_V = 128
MLA_Q_BLOCK = 128
MLA_SCALE = 1.0 / math.sqrt(MLA_NOPE + MLA_ROPE)

GMLP_GROUPS = 8
GMLP_GROUP_DIM = 128
GMLP_CHUNK = 128

EVEN_SPLITS = (MLA_Q_LORA, MLA_Q_LORA + MLA_KV_LORA, MLA_Q_LORA + MLA_KV_LORA + MLA_ROPE)
EVEN_IN = EVEN_SPLITS[-1] + 2 * GMLP_GROUPS * GMLP_GROUP_DIM
EVEN_MIX = MLA_HEADS * MLA_V + GMLP_GROUPS * GMLP_GROUP_DIM

SWA_HEADS = 32
SWA_KV_HEADS = 4
SWA_GROUP = SWA_HEADS // SWA_KV_HEADS
SWA_HEAD_DIM = 64
SWA_WINDOW = 128
SWA_BLOCK = 128
SWA_SPAN = SWA_BLOCK + 2 * SWA_WINDOW
SWA_SCALE = 1.0 / math.sqrt(SWA_HEAD_DIM)
ODD_IN = (SWA_HEADS + 2 * SWA_KV_HEADS) * SWA_HEAD_DIM
ODD_MIX = SWA_HEADS * SWA_HEAD_DIM

kernel_name = 'hybrid_mla_gmlp_swa_dit_block'


def rms_norm(x, g):
    xf = x.astype(jnp.float32)
    y = xf * lax.rsqrt(jnp.mean(jnp.square(xf), axis=-1, keepdims=True) + NORM_EPS)
    return (y * g.astype(jnp.float32)).astype(x.dtype)


def group_layer_norm(x):
    xf = x.astype(jnp.float32)
    mu = jnp.mean(xf, axis=-1, keepdims=True)
    xc = xf - mu
    var = jnp.mean(jnp.square(xc), axis=-1, keepdims=True)
    return (xc * lax.rsqrt(var + NORM_EPS)).astype(x.dtype)


def axial_rope(n_tokens, rot_dim):
    rows = n_tokens // GRID_W
    t = jnp.arange(rows * GRID_W)
    row = (t // GRID_W).astype(jnp.float32)
    col = (t % GRID_W).astype(jnp.float32)
    n_freq = rot_dim // 4
    inv_freq = ROPE_BASE ** (-jnp.arange(n_freq, dtype=jnp.float32) / n_freq)
    ang = jnp.concatenate([row[:, None] * inv_freq, col[:, None] * inv_freq], axis=-1)
    return jnp.cos(ang), jnp.sin(ang)


def apply_rope(x, cos, sin):
    half = x.shape[-1] // 2
    x1 = x[..., :half].astype(jnp.float32)
    x2 = x[..., half:].astype(jnp.float32)
    return jnp.concatenate([x1 * cos - x2 * sin, x1 * sin + x2 * cos], axis=-1).astype(x.dtype)


def modulate(h, shift, scale):
    return h * (1 + scale) + shift


def squared_relu_mlp(h, w1, w2):
    return jnp.square(jax.nn.relu(h @ w1)) @ w2


def mla_attend(q_nope, q_rope, k_nope, k_rope, v):
    s = jnp.einsum('bqhd,bkhd->bhqk', q_nope, k_nope) + jnp.einsum('bqhr,bkr->bhqk', q_rope, k_rope)
    p = jax.nn.softmax(s.astype(jnp.float32) * MLA_SCALE, axis=-1)
    return jnp.einsum('bhqk,bkhd->bqhd', p.astype(v.dtype), v)


def chunk_gmlp(gm, w_sp, b_sp):
    b, n = gm.shape[:2]
    z = jax.nn.gelu(gm)
    u, v = jnp.split(z, 2, axis=-1)
    v = group_layer_norm(v.reshape(b, n // GMLP_CHUNK, GMLP_CHUNK, GMLP_GROUPS, GMLP_GROUP_DIM))
    mixed = jnp.einsum('gpq,bnqgc->bnpgc', w_sp, v) + b_sp.T[:, :, None]
    return u * mixed.reshape(b, n, GMLP_GROUPS * GMLP_GROUP_DIM)


def mla_gmlp_mixer(h_lat, h_ctx, w_in, q_norm_g, w_uq, kv_norm_g, w_ukv, w_sp, b_sp, w_out,
                   cos, sin, with_ctx_out):
    def project(h):
        b, n = h.shape[:2]
        z = h @ w_in
        cq, ckv, k_rope, gm = jnp.split(z, EVEN_SPLITS, axis=-1)
        q = (rms_norm(cq, q_norm_g) @ w_uq).reshape(b, n, MLA_HEADS, MLA_NOPE + MLA_ROPE)
        kv = (rms_norm(ckv, kv_norm_g) @ w_ukv).reshape(b, n, MLA_HEADS, MLA_NOPE + MLA_V)
        return q[..., :MLA_NOPE], q[..., MLA_NOPE:], kv[..., :MLA_NOPE], k_rope, kv[..., MLA_NOPE:], gm

    qn_l, qr_l, kn_l, kr_l, v_l, gm_l = project(h_lat)
    qn_c, qr_c, kn_c, kr_c, v_c, gm_c = project(h_ctx)
    qr_l = apply_rope(qr_l, cos[:, None, :], sin[:, None, :])
    kr_l = apply_rope(kr_l, cos, sin)
    kn_all = jnp.concatenate([kn_l, kn_c], axis=1)
    kr_all = jnp.concatenate([kr_l, kr_c], axis=1)
    v_all = jnp.concatenate([v_l, v_c], axis=1)
    b, n = h_lat.shape[:2]

    def q_block(i):
        start = i * MLA_Q_BLOCK
        qn = lax.dynamic_slice_in_dim(qn_l, start, MLA_Q_BLOCK, axis=1)
        qr = lax.dynamic_slice_in_dim(qr_l, start, MLA_Q_BLOCK, axis=1)
        return mla_attend(qn, qr, kn_all, kr_all, v_all)

    att_l = lax.map(q_block, jnp.arange(n // MLA_Q_BLOCK))
    att_l = jnp.moveaxis(att_l, 0, 1).reshape(b, n, MLA_HEADS * MLA_V)
    y_lat = jnp.concatenate([att_l, chunk_gmlp(gm_l, w_sp, b_sp)], axis=-1) @ w_out
    if not with_ctx_out:
        return y_lat, None
    att_c = mla_attend(qn_c, qr_c, kn_c, kr_c, v_c).reshape(b, h_ctx.shape[1], MLA_HEADS * MLA_V)
    y_ctx = jnp.concatenate([att_c, chunk_gmlp(gm_c, w_sp, b_sp)], axis=-1) @ w_out
    return y_lat, y_ctx


def sink_attend(q, k, v, sink, mask):
    s = jnp.einsum('bqkgd,bjkd->bkgqj', q, k).astype(jnp.float32) * SWA_SCALE
    if mask is not None:
        s = jnp.where(mask, s, NEG_INF)
    sk = jnp.broadcast_to(sink.astype(jnp.float32)[None, :, :, None, None], s.shape[:-1] + (1,))
    p = jax.nn.softmax(jnp.concatenate([s, sk], axis=-1), axis=-1)[..., :-1]
    return jnp.einsum('bkgqj,bjkd->bqkgd', p.astype(v.dtype), v)


def window_gqa_mixer(h_lat, h_ctx, w_in, sinks, w_out, cos, sin, with_ctx_out):
    def project(h):
        b, n = h.shape[:2]
        z = h @ w_in
        q, k, v = jnp.split(z, (SWA_HEADS * SWA_HEAD_DIM, (SWA_HEADS + SWA_KV_HEADS) * SWA_HEAD_DIM), axis=-1)
        return (q.reshape(b, n, SWA_KV_HEADS, SWA_GROUP, SWA_HEAD_DIM),
                k.reshape(b, n, SWA_KV_HEADS, SWA_HEAD_DIM),
                v.reshape(b, n, SWA_KV_HEADS, SWA_HEAD_DIM))

    q_l, k_l, v_l = project(h_lat)
    q_c, k_c, v_c = project(h_ctx)
    q_l = apply_rope(q_l, cos[:, None, None, :], sin[:, None, None, :])
    k_l = apply_rope(k_l, cos[:, None, :], sin[:, None, :])
    sink = sinks.reshape(SWA_KV_HEADS, SWA_GROUP)
    b, n = h_lat.shape[:2]
    n_ctx = h_ctx.shape[1]
    pad = ((0, 0), (SWA_WINDOW, SWA_WINDOW), (0, 0), (0, 0))
    k_pad = jnp.pad(k_l, pad)
    v_pad = jnp.pad(v_l, pad)
    qi = jnp.arange(SWA_BLOCK)[:, None]
    kj = jnp.arange(SWA_SPAN)[None, :]
    in_window = jnp.abs(kj - SWA_WINDOW - qi) <= SWA_WINDOW
    ctx_mask = jnp.ones((SWA_BLOCK, n_ctx), dtype=bool)

    def band_block(i):
        start = i * SWA_BLOCK
        qb = lax.dynamic_slice_in_dim(q_l, start, SWA_BLOCK, axis=1)
        kb = lax.dynamic_slice_in_dim(k_pad, start, SWA_SPAN, axis=1)
        vb = lax.dynamic_slice_in_dim(v_pad, start, SWA_SPAN, axis=1)
        pos = start - SWA_WINDOW + kj
        band = in_window & (pos >= 0) & (pos < n)
        mask = jnp.concatenate([band, ctx_mask], axis=1)
        return sink_attend(qb, jnp.concatenate([kb, k_c], axis=1), jnp.concatenate([vb, v_c], axis=1), sink, mask)

    out_l = lax.map(band_block, jnp.arange(n // SWA_BLOCK))
    out_l = jnp.moveaxis(out_l, 0, 1).reshape(b, n, ODD_MIX)
    y_lat = out_l @ w_out
    if not with_ctx_out:
        return y_lat, None
    out_c = sink_attend(q_c, k_c, v_c, sink, None).reshape(b, n_ctx, ODD_MIX)
    return y_lat, out_c @ w_out


def setup_inputs(seed: int = 0) -> dict:
    key = jax.random.key(seed)
    keys = iter(jax.random.split(key, 32))
    f32 = jnp.float32

    def normal(shape, scale):
        return jax.random.normal(next(keys), shape, f32) * scale

    def dense(shape, fan_in, scale=1.0):
        return normal(shape, scale * fan_in ** -0.5)

    def gain(shape):
        return 1.0 + normal(shape, 0.02)

    inputs = {}
    inputs['x'] = normal((BATCH, SEQ, D_MODEL), 1.0)
    inputs['c'] = normal((BATCH, D_MODEL), 1.0)
    inputs['ctx'] = normal((BATCH, CTX_LEN, D_MODEL), 1.0)
    inputs['c_ctx'] = normal((D_MODEL,), 1.0)
    inputs['norm1_g'] = gain((DEPTH, D_MODEL))
    inputs['w_mod'] = dense((DEPTH, D_MODEL, N_MOD * D_MODEL), D_MODEL, 0.5)
    inputs['b_mod'] = normal((DEPTH, N_MOD * D_MODEL), 0.02)
    inputs['norm2_g'] = gain((DEPTH, D_MODEL))
    inputs['w_ff1'] = dense((DEPTH, D_MODEL, D_FF), D_MODEL)
    inputs['w_ff2'] = dense((DEPTH, D_FF, D_MODEL), D_FF)
    inputs['even_w_in'] = dense((N_EVEN, D_MODEL, EVEN_IN), D_MODEL)
    inputs['mla_q_norm_g'] = gain((N_EVEN, MLA_Q_LORA))
    inputs['mla_w_uq'] = dense((N_EVEN, MLA_Q_LORA, MLA_HEADS * (MLA_NOPE + MLA_ROPE)), MLA_Q_LORA)
    inputs['mla_kv_norm_g'] = gain((N_EVEN, MLA_KV_LORA))
    inputs['mla_w_ukv'] = dense((N_EVEN, MLA_KV_LORA, MLA_HEADS * (MLA_NOPE + MLA_V)), MLA_KV_LORA)
    inputs['gmlp_w_sp'] = dense((N_EVEN, GMLP_GROUPS, GMLP_CHUNK, GMLP_CHUNK), GMLP_CHUNK, 0.5)
    inputs['gmlp_b_sp'] = gain((N_EVEN, GMLP_GROUPS, GMLP_CHUNK))
    inputs['even_w_out'] = dense((N_EVEN, EVEN_MIX, D_MODEL), EVEN_MIX)
    inputs['odd_w_in'] = dense((N_ODD, D_MODEL, ODD_IN), D_MODEL)
    inputs['swa_sinks'] = normal((N_ODD, SWA_HEADS), 0.5)
    inputs['odd_w_out'] = dense((N_ODD, ODD_MIX, D_MODEL), ODD_MIX)
    inputs['final_norm_g'] = gain((D_MODEL,))
    return inputs


def reference(x, c, ctx, c_ctx, norm1_g, w_mod, b_mod, norm2_g, w_ff1, w_ff2,
              even_w_in, mla_q_norm_g, mla_w_uq, mla_kv_norm_g, mla_w_ukv, gmlp_w_sp, gmlp_b_sp,
              even_w_out, odd_w_in, swa_sinks, odd_w_out, final_norm_g):
    n = x.shape[1]
    cos_m, sin_m = axial_rope(n, MLA_ROPE)
    cos_s, sin_s = axial_rope(n, SWA_HEAD_DIM)
    silu_c = jax.nn.silu(c)
    silu_cc = jax.nn.silu(c_ctx)
    xl, xc = x, ctx
    for layer in range(DEPTH):
        need_ctx = layer < DEPTH - 1
        j = layer // 2
        mod_l = jnp.split(silu_c @ w_mod[layer] + b_mod[layer], N_MOD, axis=-1)
        sh1, sc1, g1, sh2, sc2, g2 = [m[:, None, :] for m in mod_l]
        csh1, csc1, cg1, csh2, csc2, cg2 = jnp.split(silu_cc @ w_mod[layer] + b_mod[layer], N_MOD, axis=-1)
        hl = modulate(rms_norm(xl, norm1_g[layer]), sh1, sc1)
        hc = modulate(rms_norm(xc, norm1_g[layer]), csh1, csc1)
        if layer % 2 == 0:
            yl, yc = mla_gmlp_mixer(hl, hc, even_w_in[j], mla_q_norm_g[j], mla_w_uq[j], mla_kv_norm_g[j],
                                    mla_w_ukv[j], gmlp_w_sp[j], gmlp_b_sp[j], even_w_out[j],
                                    cos_m, sin_m, need_ctx)
        else:
            yl, yc = window_gqa_mixer(hl, hc, odd_w_in[j], swa_sinks[j], odd_w_out[j],
                                      cos_s, sin_s, need_ctx)
        xl = xl + g1 * yl
        hl = modulate(rms_norm(xl, norm2_g[layer]), sh2, sc2)
        xl = xl + g2 * squared_relu_mlp(hl, w_ff1[layer], w_ff2[layer])
        if need_ctx:
            xc = xc + cg1 * yc
            hc = modulate(rms_norm(xc, norm2_g[layer]), csh2, csc2)
            xc = xc + cg2 * squared_relu_mlp(hc, w_ff1[layer], w_ff2[layer])
    return rms_norm(xl, final_norm_g)
```

```python
import math
from contextlib import ExitStack

import numpy as np
import concourse.bass as bass
import concourse.mybir as mybir
from concourse.bass_utils import run_bass_kernel_spmd

F32 = mybir.dt.float32
BF16 = mybir.dt.bfloat16
AF = mybir.ActivationFunctionType
ALU = mybir.AluOpType
AX = mybir.AxisListType

P = 128
D = 2048
DC = 16
NL = 2048
NX = 256
NT = NL + NX
DFF = 8192
EPS = 1e-6
MLA_SCALE = 1.0 / math.sqrt(192.0)
SWA_SCALE = 0.125
NEG = -30000.0
TILES = [(0, 512, 0), (512, 512, 0), (1024, 512, 0), (1536, 512, 0), (2048, 256, 1)]


class Res:
    __slots__ = ("name", "lw", "rd")

    ALL = []

    def __init__(self, name):
        self.name = name
        self.lw = None
        self.rd = []
        Res.ALL.append(self)


class _Op:
    __slots__ = ("eng", "fn", "deps", "key", "val", "signal", "idx", "amt", "ext")


class Sched:
    ENGS = ("pe", "act", "dve", "pool", "sp")

    def __init__(self, nc, stack):
        self.nc = nc
        self.stack = stack
        self.sems = {}
        self.ops = []
        self.phase_keys = set()
        self.nphase = 0

    def _sem(self, name):
        if name not in self.sems:
            h = self.stack.enter_context(self.nc.semaphore("s_" + name))
            self.sems[name] = [h, 0]
        return self.sems[name]

    def begin(self):
        self.ops = []
        self.phase_keys = set()
        for r in Res.ALL:
            r.lw = None
            r.rd = []

    def op(self, eng, fn, reads=(), writes=(), key=None, amt=16, ext=(), nodrain=False):
        o = _Op()
        o.amt = amt
        o.ext = list(ext)
        o.eng = eng
        o.fn = fn
        o.key = key
        o.signal = False
        o.val = None
        o.idx = len(self.ops)
        deps = set()
        for r in reads:
            if r.lw is not None:
                deps.add(r.lw)
        for w in writes:
            if w.lw is not None:
                deps.add(w.lw)
            deps.update(w.rd)
        for r in reads:
            r.rd.append(o.idx)
        for w in writes:
            w.lw = o.idx
            w.rd = []
        deps.discard(o.idx)
        o.deps = sorted(deps)
        if key is not None:
            s = self._sem("d_" + key)
            s[1] += amt
            o.val = s[1]
            if not nodrain:
                self.phase_keys.add(key)
        self.ops.append(o)
        return o

    def end(self, name=None):
        ops = self.ops
        for o in ops:
            for d in o.deps:
                do = ops[d]
                if do.key is not None:
                    continue
                if do.eng == "pe" and o.eng == "pe" and o.key is None:
                    continue
                do.signal = True
        for o in ops:
            if o.key is None and o.signal:
                s = self._sem("e_" + o.eng)
                s[1] += 1
                o.val = s[1]
        streams = {e: [] for e in self.ENGS}
        known = {e: {} for e in self.ENGS}
        for o in ops:
            waits = {}
            for d in o.deps:
                do = ops[d]
                if do.key is not None:
                    sn = "d_" + do.key
                else:
                    if do.eng == "pe" and o.eng == "pe" and o.key is None:
                        continue
                    sn = "e_" + do.eng
                if do.val > waits.get(sn, 0):
                    waits[sn] = do.val
            for sn, v in o.ext:
                if v > waits.get(sn, 0):
                    waits[sn] = v
            wl = []
            for sn, v in waits.items():
                if known[o.eng].get(sn, 0) >= v:
                    continue
                known[o.eng][sn] = v
                wl.append((self.sems[sn][0], v))
            streams[o.eng].append((o, wl))
        fin = []
        for k in sorted(self.phase_keys):
            s = self.sems["d_" + k]
            fin.append((s[0], s[1]))

        def run(eng_name, e):
            for o, wl in streams[eng_name]:
                for h, v in wl:
                    e.wait_ge(h, v)
                ins = o.fn(e)
                if o.key is not None:
                    ins.then_inc(self.sems["d_" + o.key][0], o.amt)
                elif o.signal:
                    ins.then_inc(self.sems["e_" + o.eng][0], 1)
            if eng_name == "sp":
                for h, v in fin:
                    e.wait_ge(h, v)

        self.nphase += 1
        with self.nc.Block() as block:
            if streams["pe"]:
                block.tensor(lambda e: run("pe", e))
            if streams["act"]:
                block.scalar(lambda e: run("act", e))
            if streams["dve"]:
                block.vector(lambda e: run("dve", e))
            if streams["pool"]:
                block.gpsimd(lambda e: run("pool", e))
            block.sync(lambda e: run("sp", e))
        self.ops = []


class Ring:
    def __init__(self, views, name):
        self.views = views
        self.res = [Res(f"{name}{i}") for i in range(len(views))]
        self.i = -1

    def next(self):
        self.i = (self.i + 1) % len(self.views)
        return self.views[self.i], self.res[self.i]


def bcast_ap(ap, shape_steps):
    return bass.AP(ap.tensor, ap.offset, [list(ap.ap[0])] + [list(x) for x in shape_steps])


ARENA_BYTES = 200704
HT_BYTES = DC * NT * 2
TOP0 = ARENA_BYTES - 41472
TOP1 = ARENA_BYTES - 36864

WSPEC = [("w_in0", D, 3200), ("w_q", 512, 2048), ("w_kv", 512, 2048), ("w_out0", D, D),
         ("w_ff1_0", D, DFF), ("w_ff2_0", DFF, D), ("w_in1", D, 5376), ("w_out1", D, D),
         ("w_ff1_1", D, DFF), ("w_ff2_1", DFF, D)]


class Prog:
    def __init__(self, ncores=8, debug=None, stop=None):
        self.ncores = ncores
        self.debug = debug or set()
        self.stop = stop
        self.nc = bass.Bass("TRN2", target_bir_lowering=False)
        self.gstack = ExitStack()
        self.S = Sched(self.nc, self.gstack)
        self.wready = {}

    def din(self, name, shape, dt=F32):
        return self.nc.dram_tensor(name, list(shape), dt, kind="ExternalInput").ap()

    def dout(self, name, shape, dt=F32):
        return self.nc.dram_tensor(name, list(shape), dt, kind="ExternalOutput").ap()

    def dscr(self, name, shape, dt):
        if "scr" in self.debug:
            return self.nc.dram_tensor(name, list(shape), dt, kind="ExternalOutput").ap()
        return self.nc.dram_tensor(name, list(shape), dt).ap()

    def V(self, off, shape, dt):
        n = 1
        for x in shape[1:]:
            n *= x
        esz = 4 if dt == F32 else 2
        assert off % 4 == 0 and off + n * esz <= ARENA_BYTES, (off, shape)
        ap = self.arena[:, off // 2: off // 2 + n * esz // 2]
        if dt == F32:
            ap = ap.bitcast(F32)
        if len(shape) == 3:
            ap = ap.rearrange("p (a b) -> p a b", a=shape[1])
        elif len(shape) == 4:
            ap = ap.rearrange("p (a b c) -> p a b c", a=shape[1], b=shape[2])
        return ap

    def bank(self, b):
        return self.psum[:, b * 512:(b + 1) * 512]

    def dma(self, q, out, in_, key, reads=(), writes=(), ext=()):
        self.S.op(q, lambda e, out=out, in_=in_: e.dma_start(out=out, in_=in_), reads=reads, writes=writes,
                  key=key, ext=ext)

    def wext(self, name):
        return [self.wready[name]]

    def dump(self, name, src_ap, shape, dt, res):
        d = self.nc.dram_tensor("dbg_" + name, list(shape), dt, kind="ExternalOutput").ap()
        self.dma("sp", d, src_ap, "dbg_" + name, reads=[res])

    def mmgroup(self, out, pairs, reads, writes):
        def f(e):
            n = len(pairs)
            for i, (l, r) in enumerate(pairs):
                ins = e.matmul(out, l, r, start=(i == 0), stop=(i == n - 1))
            return ins
        self.S.op("pe", f, reads=reads, writes=writes)

    def mod_vec(self, l, kind, j):
        return self.modT[:, l, kind * DC:(kind + 1) * DC, j]

    def build(self):
        nc, S, g = self.nc, self.S, self.gstack
        nr = self.ncores
        self.xT = self.din("xT", [D, NT])
        self.cT = self.din("cT", [P, DC, 10])
        self.sel = self.din("sel", [P, 8])
        self.n1g = self.din("n1g", [P, 2, DC])
        self.n2g = self.din("n2g", [P, 2, DC])
        self.fng = self.din("fng", [P, DC])
        self.bmod = self.din("bmod", [P, 2, 96])
        self.w_mod = self.din("w_mod", [2, D, 6 * D // nr])
        self.qng = self.din("qng", [P, 4])
        self.kvng = self.din("kvng", [P, 4])
        self.w_spT = self.din("w_spT", [P, 8, P])
        self.b_sp = self.din("b_sp", [1, 8 * P])
        self.sinks = self.din("sinks", [1, 32])
        self.ropec = self.din("ropec", [P, NL])
        self.ropes = self.din("ropes", [P, NL])
        self.masks = self.din("masks", [P, 2, 512])
        self.ident = self.din("ident", [P, P])
        self.ws = {}
        self.wb = {}
        self.wsb = {}
        for (name, r, c) in WSPEC:
            self.ws[name] = self.din("ws_" + name, [r // nr, c])
            self.wb[name] = nc.dram_tensor("wb_" + name, [r, c], BF16)
            if nr > 1:
                self.wsb[name] = nc.dram_tensor("wsb_" + name, [r // nr, c], BF16)
        self.outT = self.dout("outT", [D, NL])
        self.xa = self.dscr("xa", [D, NT], F32)
        self.xb = self.dscr("xb", [D, NT], F32)
        self.u_d = self.dscr("u_d", [P, 8, NT], BF16)
        self.vn_d = self.dscr("vn_d", [18, P, 1024], BF16)
        self.q1_d = self.dscr("q1_d", [P, 16, NL], BF16)
        self.mcpr = 96 // nr
        self.mod_sh = nc.dram_tensor("mod_sh", [P, 2 * self.mcpr * 10], F32)
        self.mod_all = nc.dram_tensor("mod_all", [nr * P, 2 * self.mcpr * 10], F32)

        sbt = lambda name, shape, dt: g.enter_context(nc.sbuf_tensor(name, list(shape), dt))
        self.modT = sbt("modT", [P, 2, 96, 2], F32)
        self.A1 = sbt("A1", [P, 2, DC, 2], F32)
        self.A2 = sbt("A2", [P, 2, DC, 2], F32)
        self.fng_s = sbt("fng_s", [P, DC], F32)
        self.qng_s = sbt("qng_s", [P, 4], F32)
        self.kvng_s = sbt("kvng_s", [P, 4], F32)
        self.ones_d = sbt("ones_d", [P, P], BF16)
        self.ones_q = sbt("ones_q", [P, P], BF16)
        self.ones_1 = sbt("ones_1", [P, P], BF16)
        self.eps_t = sbt("eps_t", [P, 1], F32)
        self.sbf_g = sbt("sbf_g", [P, DC, 10], BF16)
        self.sel_g = sbt("sel_g", [P, 8], F32)
        self.bm_g = sbt("bm_g", [P, 2, 96], F32)
        self.n1s_g = sbt("n1s_g", [P, 2, DC], F32)
        self.n2s_g = sbt("n2s_g", [P, 2, DC], F32)
        self.Rt_g = sbt("Rt_g", [P, 96, 10], F32)
        self.tmp_g = sbt("tmp_g", [P, 32, 8], F32)
        self.r_sbf, self.r_small, self.r_R, self.r_mod, self.r_tmpg = (Res(x) for x in ("sbf", "small", "R", "modT", "tmpg"))
        self.arena = sbt("arena", [P, ARENA_BYTES // 2], BF16)
        self.psum = g.enter_context(nc.psum_tensor("psum", [P, 8 * 512], F32))
        self.r_const = Res("const")

        phases = [
            ("mod", lambda: self.phase_mod()),
            ("n1_0", lambda: self.phase_norm(self.xT, 0, self.A1, 0, TILES)),
            ("p0", lambda: self.phase_p0()),
            ("att0", lambda: self.phase_att0()),
            ("g0", lambda: self.phase_g0()),
            ("o0", lambda: self.phase_o(0)),
            ("n2_0", lambda: self.phase_norm(self.xa, 0, self.A2, 3, TILES)),
            ("f0", lambda: self.phase_ffn(0)),
            ("n1_1", lambda: self.phase_norm(self.xb, 1, self.A1, 0, TILES)),
            ("p1", lambda: self.phase_p1()),
            ("att1", lambda: self.phase_att1()),
            ("o1", lambda: self.phase_o(1)),
            ("n2_1", lambda: self.phase_norm(self.xa, 1, self.A2, 3, TILES[:4])),
            ("f1", lambda: self.phase_ffn(1)),
        ]
        for name, fn in phases:
            fn()
            if self.stop == name:
                break
        self.gstack.close()

    def phase_mod(self):
        S = self.S
        assert self.ncores == 1
        S.begin()
        cin = self.V(0, [P, DC, 10], F32)
        slab = [self.V(4096 + i * 12288, [P, DC, 384], BF16) for i in range(2)]
        r_cin = Res("cin")
        S.op("dve", lambda e: e.memset(self.ones_d[:], 1.0 / D), writes=[self.r_const])
        S.op("dve", lambda e: e.memset(self.ones_q[:], 1.0 / 512), writes=[self.r_const])
        S.op("dve", lambda e: e.memset(self.ones_1[:], 1.0), writes=[self.r_const])
        S.op("dve", lambda e: e.memset(self.eps_t[:], EPS), writes=[self.r_const])
        self.dma("sp", cin, self.cT, "cin", writes=[r_cin])
        self.dma("sp", self.n1s_g[:], self.n1g, "sm1", writes=[self.r_small])
        self.dma("sp", self.n2s_g[:], self.n2g, "sm2", writes=[self.r_small])
        self.dma("sp", self.bm_g[:], self.bmod, "sm3", writes=[self.r_small])
        self.dma("sp", self.sel_g[:], self.sel, "sm4", writes=[self.r_small])
        self.dma("sp", self.fng_s[:], self.fng, "sm5", writes=[self.r_const])
        self.dma("sp", self.qng_s[:], self.qng, "sm6", writes=[self.r_const])
        self.dma("sp", self.kvng_s[:], self.kvng, "sm7", writes=[self.r_const])
        S.op("act", lambda e: e.activation(self.sbf_g[:], cin, AF.Silu), reads=[r_cin], writes=[self.r_sbf])
        self.prologue(["w_in0", "w_q", "w_kv"])
        ring = Ring(slab, "mslab")
        pring = Ring([self.bank(0)[:, 0:30], self.bank(1)[:, 0:30]], "pmod")
        jobs = self.mod_jobs(0, 0, 11, ring, pring)
        for j in jobs:
            j()
        self.mod_finish(0, 0, 32)
        if "mod" in self.debug:
            self.dump("modT", self.modT[:], [P, 2, 96, 2], F32, self.r_mod)
        S.end("mod")

    def mod_jobs(self, l, s0, s1, ring, pring):
        S = self.S
        wv = self.w_mod[l].rearrange("(k p) n -> p k n", p=P)
        st = {}

        def load(s):
            buf, rb = ring.next()
            st[s] = (buf, rb)
            self.dma("pool", buf, wv[:, :, s * 384:(s + 1) * 384], f"{ring.res[ring.i].name}", writes=[rb])

        def comp(s):
            buf, rb = st.pop(s)
            pb, rpb = pring.next()
            for mm in range(3):
                self.mmgroup(pb[:, mm * 10:(mm + 1) * 10],
                             [(buf[:, k, mm * P:(mm + 1) * P], self.sbf_g[:, k, :]) for k in range(DC)],
                             [rb, self.r_sbf], [rpb])
            S.op("act", lambda e, pb=pb, s=s: e.activation(
                self.Rt_g[:, s * 3:(s + 1) * 3, :], pb.rearrange("p (a b) -> p a b", b=10), AF.Copy),
                reads=[rpb], writes=[self.r_R])
        jobs = []
        ss = list(range(s0, s1))
        jobs.append(lambda: [load(x) for x in ss[:2]])
        for i, s in enumerate(ss):
            jobs.append(lambda s=s, i=i: (comp(s), load(ss[i + 2]) if i + 2 < len(ss) else None))
        return jobs

    def mod_finish(self, l, m0, m1):
        S = self.S
        for c0 in range(m0, m1, 32):
            c1 = min(c0 + 32, m1)
            n = c1 - c0
            selb = bcast_ap(self.sel_g[:], [[0, n], [1, 8]])
            S.op("dve", lambda e, c0=c0, c1=c1, n=n, selb=selb: e.tensor_tensor(
                self.tmp_g[:, 0:n, :], self.Rt_g[:, c0:c1, 0:8], selb, ALU.mult),
                reads=[self.r_R, self.r_small], writes=[self.r_tmpg])
            S.op("dve", lambda e, c0=c0, c1=c1, n=n: e.tensor_reduce(self.modT[:, l, c0:c1, 0], self.tmp_g[:, 0:n, :], AX.X, ALU.add),
                 reads=[self.r_tmpg], writes=[self.r_mod])
            S.op("act", lambda e, c0=c0, c1=c1: e.activation(self.modT[:, l, c0:c1, 1], self.Rt_g[:, c0:c1, 8], AF.Copy),
                 reads=[self.r_R], writes=[self.r_mod])
            bv = bcast_ap(self.bm_g[:, l, c0:c1], [[1, n], [0, 2]])
            S.op("dve", lambda e, c0=c0, c1=c1, bv=bv: e.tensor_tensor(self.modT[:, l, c0:c1, :], self.modT[:, l, c0:c1, :], bv, ALU.add),
                 reads=[self.r_mod, self.r_small], writes=[self.r_mod])
        for (A, ns, kind) in ((self.A1, self.n1s_g, 1), (self.A2, self.n2s_g, 4)):
            if m0 <= kind * DC and (kind + 1) * DC <= m1:
                for j in range(2):
                    S.op("dve", lambda e, A=A, ns=ns, kind=kind, j=j: e.scalar_tensor_tensor(
                        A[:, l, :, j], self.mod_vec(l, kind, j), 1.0, ns[:, l, :], ALU.add, ALU.mult),
                        reads=[self.r_mod, self.r_small], writes=[self.r_const])

    def prologue(self, names, after=()):
        S, nr = self.S, self.ncores
        for (name, r, c) in WSPEC:
            if name not in names:
                continue
            if nr > 1:
                rs = Res("wsb_" + name)
                self.dma("pool", self.wsb[name].ap(), self.ws[name], "cast_" + name, writes=[rs])
                S.op("pool", lambda e, name=name: e.collective_compute(
                    "AllGather", ALU.bypass, replica_groups=[list(range(nr))],
                    ins=[self.wsb[name].ap().opt()], outs=[self.wb[name].ap().opt()]),
                    reads=[rs], key="g_" + name, amt=1, nodrain=True)
                self.wready[name] = ("d_g_" + name, 1)
            else:
                nb = 8
                rb = r // nb
                for i in range(nb):
                    S.op("pool", lambda e, name=name, i=i, rb=rb: e.dma_start(
                        out=self.wb[name].ap()[i * rb:(i + 1) * rb, :], in_=self.ws[name][i * rb:(i + 1) * rb, :]),
                        reads=list(after), key="g_" + name, nodrain=True)
                self.wready[name] = ("d_g_" + name, 16 * nb)

    def rms_core(self, sqv, r_sq, n, nch, ones, psb, r_ps, sd, r_sd):
        S = self.S
        self.mmgroup(psb[:, :n], [(ones[:], sqv[:, c, :n]) for c in range(nch)], [r_sq, self.r_const], [r_ps])
        S.op("act", lambda e: e.activation(sd[:, :n], psb[:, :n], AF.Sqrt, bias=self.eps_t[:, 0:1]),
             reads=[r_ps, self.r_const], writes=[r_sd])
        S.op("dve", lambda e: e.reciprocal(sd[:, :n], sd[:, :n]), reads=[r_sd], writes=[r_sd])

    def phase_norm(self, src, l, A, sh_kind, tiles):
        S = self.S
        S.begin()
        self.hT = self.V(0, [P, DC, NT], BF16)
        self.r_hT = Res("hT")
        srcv = src.rearrange("(c p) t -> p c t", p=P)
        T0 = HT_BYTES
        xbufs = [self.V(T0 + i * 32768, [P, DC, 512], F32) for i in range(2)]
        sq = self.V(T0 + 65536, [P, DC, 512], BF16)
        sd = self.V(T0 + 81920, [P, 512], F32)
        r_sq, r_ps, r_sd = Res("sq"), Res("ps"), Res("sd")
        psb = self.bank(0)
        ring = Ring(xbufs, "xbuf")
        A_l = A[:, l]
        sh_l = self.modT[:, l, sh_kind * DC:(sh_kind + 1) * DC, :]

        rxc = [[Res(f"xb{i}c{c}") for c in range(DC)] for i in range(2)]

        def one(pend):
            xs, rx, t0, n, j = pend
            rc = rxc[0] if rx is ring.res[0] else rxc[1]
            S.op("act", lambda e: e.activation(sq[:, :, :n], xs[:, :, :n], AF.Square), reads=[rx], writes=[r_sq] + rc)
            self.rms_core(sq, r_sq, n, DC, self.ones_d, psb, r_ps, sd, r_sd)
            for c in range(DC):
                S.op("dve", lambda e, c=c: e.scalar_tensor_tensor(xs[:, c, :n], xs[:, c, :n], A_l[:, c, j:j + 1],
                                                                  sd[:, :n], ALU.mult, ALU.mult),
                     reads=[r_sd, self.r_const], writes=[rc[c]])
                S.op("act", lambda e, c=c: e.activation(self.hT[:, c, t0:t0 + n], xs[:, c, :n], AF.Identity,
                                                        bias=sh_l[:, c, j:j + 1]),
                     reads=[rc[c], rx, self.r_const], writes=[self.r_hT])
        pend = None
        for (t0, n, j) in tiles:
            xs, rx = ring.next()
            self.dma("sp", xs[:, :, :n], srcv[:, :, t0:t0 + n], f"xbuf{ring.i}", writes=[rx])
            if pend is not None:
                one(pend)
            pend = (xs, rx, t0, n, j)
        one(pend)
        if "norm" in self.debug:
            self.dump("hT", self.hT, [P, DC, NT], BF16, self.r_hT)
        S.end("norm")

    def load_rope(self, off, S):
        cosT = self.V(off, [P, NL], F32)
        sinT = self.V(off + 8192, [P, NL], F32)
        r_rope = Res("rope")
        self.dma("sp", cosT, self.ropec, "ropec", writes=[r_rope])
        self.dma("sp", sinT, self.ropes, "ropes", writes=[r_rope])
        return cosT, sinT, r_rope

    def phase_p0(self):
        S = self.S
        S.begin()
        self.prologue(["w_out0", "w_ff1_0", "w_ff2_0"])
        hT, r_hT = self.hT, self.r_hT
        self.cqn = self.V(TOP0, [P, 4, NT], BF16)
        self.ckvn = self.V(TOP0 + 18432, [P, 4, NT], BF16)
        self.krT = self.V(TOP0 + 36864, [P, NT], BF16)
        self.r_cqn, self.r_ckvn, self.r_krT = Res("cqn"), Res("ckvn"), Res("krT")
        T0 = HT_BYTES
        slabs = [self.V(T0 + i * 16384, [P, DC, 512], BF16) for i in range(2)]
        o = T0 + 32768
        cq = self.V(o, [P, 4, 512], F32); o += 8192
        sq4 = self.V(o, [P, 4, 512], BF16); o += 4096
        sd = self.V(o, [P, 512], F32); o += 2048
        uo = [self.V(o + i * 1024, [P, 512], BF16) for i in range(2)]; o += 2048
        vg = [self.V(o + i * 2048, [P, 512], F32) for i in range(2)]; o += 4096
        vsq = self.V(o, [P, 512], F32); o += 2048
        stt = self.V(o, [P, 24], F32); o += 96
        vno = [self.V(o + i * 1024, [P, 512], BF16) for i in range(2)]; o += 2048
        cosT, sinT, r_rope = self.load_rope(o, S); o += 16384
        t1 = self.V(o, [P, 512], F32); o += 2048
        t2 = self.V(o, [P, 512], F32); o += 2048
        assert o <= TOP0
        r_cq, r_sq4, r_sd, r_vsq, r_st, r_t1, r_t2, r_pst = (Res(x) for x in "cq sq4 sd vsq st t1 t2 pst".split())
        ring = Ring(slabs, "slab")
        pring = Ring([self.bank(i) for i in range(4)], "pp")
        uring = Ring(uo, "uo")
        vgring = Ring(vg, "vg")
        vnring = Ring(vno, "vno")
        pst = self.bank(4)
        wv = self.wb["w_in0"].ap().rearrange("(k p) n -> p k n", p=P)
        ext = self.wext("w_in0")
        for s in range(7):
            ncol = 512 if s < 6 else 128
            buf, rb = ring.next()
            self.dma("sp", buf[:, :, :ncol], wv[:, :, s * 512:s * 512 + ncol], f"slab{ring.i}", writes=[rb], ext=ext)
            for (t0, n, j) in TILES:
                if s < 2:
                    dst, r_dst, gn = (self.cqn, self.r_cqn, self.qng_s) if s == 0 else (self.ckvn, self.r_ckvn, self.kvng_s)
                    for mm in range(4):
                        pb, rpb = pring.next()
                        self.mmgroup(pb[:, :n], [(buf[:, k, mm * P:(mm + 1) * P], hT[:, k, t0:t0 + n]) for k in range(DC)],
                                     [rb, r_hT], [rpb])
                        S.op("act", lambda e, pb=pb, mm=mm, n=n: e.activation(cq[:, mm, :n], pb[:, :n], AF.Copy),
                             reads=[rpb], writes=[r_cq])
                        S.op("act", lambda e, pb=pb, mm=mm, n=n: e.activation(sq4[:, mm, :n], pb[:, :n], AF.Square),
                             reads=[rpb], writes=[r_sq4])
                    self.rms_core(sq4, r_sq4, n, 4, self.ones_q, pst, r_pst, sd, r_sd)
                    for mm in range(4):
                        S.op("dve", lambda e, mm=mm, dst=dst, gn=gn, t0=t0, n=n: e.scalar_tensor_tensor(
                            dst[:, mm, t0:t0 + n], cq[:, mm, :n], gn[:, mm:mm + 1], sd[:, :n], ALU.mult, ALU.mult),
                            reads=[r_cq, r_sd, self.r_const], writes=[r_dst])
                elif s < 4:
                    for mm in range(4):
                        gidx = (s - 2) * 4 + mm
                        pb, rpb = pring.next()
                        self.mmgroup(pb[:, :n], [(buf[:, k, mm * P:(mm + 1) * P], hT[:, k, t0:t0 + n]) for k in range(DC)],
                                     [rb, r_hT], [rpb])
                        ub, rub = uring.next()
                        S.op("act", lambda e, pb=pb, ub=ub, n=n: e.activation(ub[:, :n], pb[:, :n], AF.Gelu_apprx_tanh),
                             reads=[rpb], writes=[rub])
                        self.dma("sp", self.u_d[:, gidx, t0:t0 + n], ub[:, :n], f"uo{uring.i}", reads=[rub])
                elif s < 6:
                    for jj in range(n // P):
                        ch = t0 // P + jj
                        pb, rpb = pring.next()
                        self.mmgroup(pb[:, :], [(hT[:, k, ch * P:(ch + 1) * P], buf[:, k, :]) for k in range(DC)],
                                     [rb, r_hT], [rpb])
                        vb, rvb = vgring.next()
                        S.op("act", lambda e, pb=pb, vb=vb: e.activation(vb, pb, AF.Gelu_apprx_tanh), reads=[rpb], writes=[rvb])
                        S.op("act", lambda e, vb=vb: e.activation(vsq, vb, AF.Square), reads=[rvb], writes=[r_vsq])
                        S.op("dve", lambda e, vb=vb: e.tensor_reduce(stt[:, 0:4], vb.rearrange("p (g c) -> p g c", g=4), AX.X, ALU.add),
                             reads=[rvb], writes=[r_st])
                        S.op("dve", lambda e: e.tensor_reduce(stt[:, 4:8], vsq.rearrange("p (g c) -> p g c", g=4), AX.X, ALU.add),
                             reads=[r_vsq, r_st], writes=[r_st])
                        S.op("dve", lambda e: e.tensor_scalar(stt[:, 8:12], stt[:, 0:4], 1.0 / P, None, ALU.mult), reads=[r_st], writes=[r_st])
                        S.op("dve", lambda e: e.tensor_tensor(stt[:, 12:16], stt[:, 8:12], stt[:, 8:12], ALU.mult), reads=[r_st], writes=[r_st])
                        S.op("dve", lambda e: e.scalar_tensor_tensor(stt[:, 16:20], stt[:, 4:8], 1.0 / P, stt[:, 12:16], ALU.mult, ALU.subtract),
                             reads=[r_st], writes=[r_st])
                        S.op("act", lambda e: e.activation(stt[:, 20:24], stt[:, 16:20], AF.Sqrt, bias=self.eps_t[:, 0:1]),
                             reads=[r_st, self.r_const], writes=[r_st])
                        S.op("dve", lambda e: e.reciprocal(stt[:, 20:24], stt[:, 20:24]), reads=[r_st], writes=[r_st])
                        ob, rob = vnring.next()
                        for gq in range(4):
                            S.op("dve", lambda e, gq=gq, vb=vb, ob=ob: e.tensor_scalar(
                                ob[:, gq * P:(gq + 1) * P], vb[:, gq * P:(gq + 1) * P], stt[:, 8 + gq:9 + gq], stt[:, 20 + gq:21 + gq],
                                ALU.subtract, ALU.mult), reads=[rvb, r_st], writes=[rob])
                        self.dma("sp", self.vn_d[ch, :, (s - 4) * 512:(s - 3) * 512], ob, f"vno{vnring.i}", reads=[rob])
                else:
                    pb, rpb = pring.next()
                    self.mmgroup(pb[:, :n], [(buf[:, k, 0:P], hT[:, k, t0:t0 + n]) for k in range(DC)], [rb, r_hT], [rpb])
                    if j == 0:
                        S.op("dve", lambda e, pb=pb, t0=t0, n=n: e.tensor_tensor(t1[0:64, :n], pb[64:128, :n], sinT[64:128, t0:t0 + n], ALU.mult),
                             reads=[rpb, r_rope], writes=[r_t1])
                        S.op("dve", lambda e, pb=pb, t0=t0, n=n: e.tensor_tensor(t2[0:64, :n], pb[0:64, :n], cosT[0:64, t0:t0 + n], ALU.mult),
                             reads=[rpb, r_rope], writes=[r_t2])
                        S.op("dve", lambda e, t0=t0, n=n: e.tensor_tensor(self.krT[0:64, t0:t0 + n], t1[0:64, :n], t2[0:64, :n], ALU.add),
                             reads=[r_t1, r_t2], writes=[self.r_krT])
                    else:
                        S.op("act", lambda e, pb=pb, t0=t0, n=n: e.activation(self.krT[0:64, t0:t0 + n], pb[0:64, :n], AF.Copy),
                             reads=[rpb], writes=[self.r_krT])
        if "p0" in self.debug:
            self.dump("cqn", self.cqn, [P, 4, NT], BF16, self.r_cqn)
            self.dump("ckvn", self.ckvn, [P, 4, NT], BF16, self.r_ckvn)
            self.dump("krT", self.krT[0:64], [64, NT], BF16, self.r_krT)
        S.end("p0")

    def phase_att0(self):
        S = self.S
        S.begin()
        self.prologue(["w_in1", "w_out1"])
        self.attT = self.V(0, [P, 8, NT], BF16)
        self.r_attT = Res("attT")
        o = 36864
        Wq = self.V(o, [P, 4, 2048], BF16); o += 16384
        Wkv = self.V(o, [P, 4, 2048], BF16); o += 16384
        sets = []
        for i in range(2):
            sets.append(dict(QT=self.V(o, [P, NT], BF16), QrT=self.V(o + 4608, [P, NT], BF16),
                             KT=self.V(o + 9216, [P, NT], BF16), Vh=self.V(o + 13824, [P, 18, P], BF16),
                             r=Res(f"hset{i}")))
            o += 18432
        cosT, sinT, r_rope = self.load_rope(o, S); o += 16384
        t1 = self.V(o, [P, 512], F32); o += 2048
        t2 = self.V(o, [P, 512], F32); o += 2048
        PT = [self.V(o + i * 1024, [P, 512], BF16) for i in range(3)]; o += 3072
        rl = [self.V(o + i * 2048, [P, 512], F32) for i in range(2)]; o += 4096
        mslab = [self.V(o + i * 12288, [P, DC, 384], BF16) for i in range(2)]; o += 24576
        assert o <= TOP0, o
        mring = Ring(mslab, "mslabB")
        mpring = Ring([self.bank(7)[:, 0:30]], "pmodB")
        bg = []
        bg += self.mod_jobs(0, 11, 32, mring, mpring)
        bg.append(lambda: self.mod_finish(0, 32, 96))
        bg += self.mod_jobs(1, 0, 32, mring, mpring)
        bg.append(lambda: self.mod_finish(1, 0, 96))
        bg = iter(bg)

        def run_bg(k):
            for _ in range(k):
                j = next(bg, None)
                if j is not None:
                    j()
        r_w, r_t1, r_t2 = Res("w"), Res("t1"), Res("t2")
        self.dma("sp", Wq, self.wb["w_q"].ap().rearrange("(k p) n -> p k n", p=P), "wq", writes=[r_w], ext=self.wext("w_q"))
        self.dma("sp", Wkv, self.wb["w_kv"].ap().rearrange("(k p) n -> p k n", p=P), "wkv", writes=[r_w], ext=self.wext("w_kv"))
        sring = Ring([self.bank(i) for i in range(3)], "S")
        oring = Ring([self.bank(3), self.bank(4)], "O")
        lring = Ring([self.bank(5), self.bank(6)], "L")
        ptring = Ring(PT, "PT")
        rlring = Ring(rl, "rl")
        cqn, ckvn, krT = self.cqn, self.ckvn, self.krT
        for h in range(8):
            hs = sets[h % 2]
            rh = hs["r"]
            for (t0, n, j) in TILES:
                pb, rpb = sring.next()
                self.mmgroup(pb[:, :n], [(Wq[:, k, h * P:(h + 1) * P], cqn[:, k, t0:t0 + n]) for k in range(4)],
                             [r_w, self.r_cqn], [rpb])
                S.op("act", lambda e, pb=pb, hs=hs, t0=t0, n=n: e.activation(hs["QT"][:, t0:t0 + n], pb[:, :n], AF.Copy),
                     reads=[rpb], writes=[rh])
                pb, rpb = sring.next()
                self.mmgroup(pb[:, :n], [(Wq[:, k, 1024 + h * P:1024 + (h + 1) * P], cqn[:, k, t0:t0 + n]) for k in range(4)],
                             [r_w, self.r_cqn], [rpb])
                if j == 0:
                    S.op("dve", lambda e, pb=pb, t0=t0, n=n: e.tensor_tensor(t1[0:64, :n], pb[64:128, :n], sinT[64:128, t0:t0 + n], ALU.mult),
                         reads=[rpb, r_rope], writes=[r_t1])
                    S.op("dve", lambda e, pb=pb, t0=t0, n=n: e.tensor_tensor(t2[0:64, :n], pb[0:64, :n], cosT[0:64, t0:t0 + n], ALU.mult),
                         reads=[rpb, r_rope], writes=[r_t2])
                    S.op("dve", lambda e, hs=hs, t0=t0, n=n: e.tensor_tensor(hs["QrT"][0:64, t0:t0 + n], t1[0:64, :n], t2[0:64, :n], ALU.add),
                         reads=[r_t1, r_t2], writes=[rh])
                else:
                    S.op("act", lambda e, pb=pb, hs=hs, t0=t0, n=n: e.activation(hs["QrT"][0:64, t0:t0 + n], pb[0:64, :n], AF.Copy),
                         reads=[rpb], writes=[rh])
                pb, rpb = sring.next()
                self.mmgroup(pb[:, :n], [(Wkv[:, k, h * P:(h + 1) * P], ckvn[:, k, t0:t0 + n]) for k in range(4)],
                             [r_w, self.r_ckvn], [rpb])
                S.op("dve", lambda e, pb=pb, hs=hs, t0=t0, n=n: e.tensor_copy(hs["KT"][:, t0:t0 + n], pb[:, :n]),
                     reads=[rpb], writes=[rh])
            for cg in range(0, 18, 4):
                nch = min(4, 18 - cg)
                pb, rpb = sring.next()

                def fv(e, pb=pb, cg=cg, nch=nch, h=h):
                    for jj in range(nch):
                        ch = cg + jj
                        for k in range(4):
                            ins = e.matmul(pb[:, jj * P:(jj + 1) * P], ckvn[:, k, ch * P:(ch + 1) * P],
                                           Wkv[:, k, 1024 + h * P:1024 + (h + 1) * P], start=(k == 0), stop=(k == 3))
                    return ins
                S.op("pe", fv, reads=[r_w, self.r_ckvn], writes=[rpb])
                S.op("act", lambda e, pb=pb, hs=hs, cg=cg, nch=nch: e.activation(
                    hs["Vh"][:, cg:cg + nch, :], pb[:, :nch * P].rearrange("p (a b) -> p a b", b=P), AF.Copy),
                    reads=[rpb], writes=[rh])
            for (t0, n, j) in TILES:
                run_bg(2)
                keys = list(range(18)) if j == 0 else [16, 17]
                ob, rob = oring.next()
                lb, rlb = lring.next()

                def s_op(kc):
                    pb, rpb = sring.next()
                    self.mmgroup(pb[:, :n], [(hs["KT"][:, kc * P:(kc + 1) * P], hs["QT"][:, t0:t0 + n]),
                                             (krT[0:64, kc * P:(kc + 1) * P], hs["QrT"][0:64, t0:t0 + n])],
                                 [rh, self.r_krT], [rpb])
                    pt, rpt = ptring.next()
                    S.op("act", lambda e, pb=pb, pt=pt, n=n: e.activation(pt[:, :n], pb[:, :n], AF.Exp, scale=MLA_SCALE),
                         reads=[rpb], writes=[rpt])
                    return pt, rpt
                cur = s_op(keys[0])
                for ki, kc in enumerate(keys):
                    nxt = s_op(keys[ki + 1]) if ki + 1 < len(keys) else None
                    pt, rpt = cur

                    def fpv(e, pt=pt, kc=kc, ki=ki, ob=ob, lb=lb, last=(ki == len(keys) - 1), hs=hs, n=n):
                        e.matmul(ob[:, :n], hs["Vh"][:, kc, :], pt[:, :n], start=(ki == 0), stop=last)
                        return e.matmul(lb[:, :n], self.ones_1[:], pt[:, :n], start=(ki == 0), stop=last)
                    S.op("pe", fpv, reads=[rpt, rh, self.r_const], writes=[rob, rlb])
                    cur = nxt
                rb_, rrl = rlring.next()
                S.op("dve", lambda e, rb_=rb_, lb=lb, n=n: e.reciprocal(rb_[:, :n], lb[:, :n]), reads=[rlb], writes=[rrl])
                S.op("dve", lambda e, rb_=rb_, ob=ob, h=h, t0=t0, n=n: e.tensor_tensor(self.attT[:, h, t0:t0 + n], ob[:, :n], rb_[:, :n], ALU.mult),
                     reads=[rob, rrl], writes=[self.r_attT])
        run_bg(1000)
        if "att0" in self.debug:
            self.dump("attT", self.attT, [P, 8, NT], BF16, self.r_attT)
        S.end("att0")

    def phase_g0(self):
        S = self.S
        S.begin()
        self.gatedT = self.V(36864, [P, 8, NT], BF16)
        self.r_gat = Res("gated")
        o = HT_BYTES
        wspf = self.V(o, [P, 8, P], F32); o += 4096
        wsp = self.V(o, [P, 8, P], BF16); o += 2048
        bsp = self.V(o, [P, 1024], F32); o += 4096
        vn = [self.V(o + i * 8192, [P, 4, 1024], BF16) for i in range(2)]; o += 16384
        ut = [self.V(o + i * 8192, [P, 8, 512], BF16) for i in range(2)]; o += 16384
        tmp = [self.V(o + i * 2048, [P, 512], F32) for i in range(2)]; o += 4096
        r_w, r_b = Res("wsp"), Res("bsp")
        self.dma("sp", wspf, self.w_spT, "wspf", writes=[r_w])
        S.op("act", lambda e: e.activation(wsp, wspf, AF.Copy), reads=[r_w], writes=[r_w])
        self.dma("sp", bsp, self.b_sp.partition_broadcast(P)[:, 0, :], "bsp", writes=[r_b])
        vring, uring, tring = Ring(vn, "vn"), Ring(ut, "ut"), Ring(tmp, "tmp")
        pring = Ring([self.bank(i) for i in range(4)], "pg")
        for (t0, n, j) in TILES:
            nch = n // P
            c0 = t0 // P
            vb, rv = vring.next()
            self.dma("sp", vb[:, :nch, :], self.vn_d[c0:c0 + nch].rearrange("c p f -> p c f"), f"vn{vring.i}", writes=[rv])
            ub, ru = uring.next()
            self.dma("sp", ub[:, :, :n], self.u_d[:, :, t0:t0 + n], f"ut{uring.i}", writes=[ru])
            for g in range(8):
                pb, rpb = pring.next()

                def f(e, pb=pb, vb=vb, g=g, nch=nch):
                    for jj in range(nch):
                        ins = e.matmul(pb[:, jj * P:(jj + 1) * P], vb[:, jj, g * P:(g + 1) * P], wsp[:, g, :], start=True, stop=True)
                    return ins
                S.op("pe", f, reads=[rv, r_w], writes=[rpb])
                tb, rt = tring.next()
                bb = bcast_ap(bsp[:, g * P:(g + 1) * P], [[0, nch], [1, P]])
                S.op("dve", lambda e, pb=pb, tb=tb, bb=bb, nch=nch, n=n: e.tensor_tensor(
                    tb[:, :n].rearrange("p (a b) -> p a b", b=P), pb[:, :n].rearrange("p (a b) -> p a b", b=P), bb, ALU.add),
                    reads=[rpb, r_b], writes=[rt])
                S.op("dve", lambda e, tb=tb, ub=ub, g=g, t0=t0, n=n: e.tensor_tensor(
                    self.gatedT[:, g, t0:t0 + n], tb[:, :n], ub[:, g, :n], ALU.mult), reads=[rt, ru], writes=[self.r_gat])
        if "g0" in self.debug:
            self.dump("gatedT", self.gatedT, [P, 8, NT], BF16, self.r_gat)
        S.end("g0")

    def phase_o(self, l):
        S = self.S
        S.begin()
        if l == 0:
            mix = lambda k: self.attT[:, k] if k < 8 else self.gatedT[:, k - 8]
            rmix = [self.r_attT, self.r_gat]
            src, tiles, wname = self.xT, TILES, "w_out0"
        else:
            mix = lambda k: self.attT1[:, k]
            rmix = [self.r_attT1]
            src, tiles, wname = self.xb, TILES[:4], "w_out1"
        srcv = src.rearrange("(c p) t -> p c t", p=P)
        dstv = self.xa.rearrange("(c p) t -> p c t", p=P)
        o = HT_BYTES
        slabs = [self.V(o + i * 16384, [P, DC, 512], BF16) for i in range(2)]; o += 32768
        xp = [self.V(o + i * 8192, [P, 4, 512], F32) for i in range(2)]; o += 16384
        xo = [self.V(o + i * 8192, [P, 4, 512], F32) for i in range(2)]; o += 16384
        ring, xpr, xor_ = Ring(slabs, "slab"), Ring(xp, "xp"), Ring(xo, "xo")
        pring = Ring([self.bank(i) for i in range(4)], "po")
        wv = self.wb[wname].ap().rearrange("(k p) n -> p k n", p=P)
        for s in range(4):
            buf, rb = ring.next()
            self.dma("sp", buf, wv[:, :, s * 512:(s + 1) * 512], f"slab{ring.i}", writes=[rb], ext=self.wext(wname))
            for (t0, n, j) in tiles:
                xi, rxi = xpr.next()
                self.dma("sp", xi[:, :, :n], srcv[:, 4 * s:4 * s + 4, t0:t0 + n], f"xp{xpr.i}", writes=[rxi])
                xq, rxq = xor_.next()
                for mm in range(4):
                    m = 4 * s + mm
                    pb, rpb = pring.next()
                    self.mmgroup(pb[:, :n], [(buf[:, k, mm * P:(mm + 1) * P], mix(k)[:, t0:t0 + n]) for k in range(DC)],
                                 [rb] + rmix, [rpb])
                    g1 = self.modT[:, l, 2 * DC + m, j:j + 1]
                    S.op("dve", lambda e, pb=pb, xi=xi, xq=xq, mm=mm, g1=g1, n=n: e.scalar_tensor_tensor(
                        xq[:, mm, :n], pb[:, :n], g1, xi[:, mm, :n], ALU.mult, ALU.add),
                        reads=[rpb, rxi, self.r_const], writes=[rxq])
                self.dma("sp", dstv[:, 4 * s:4 * s + 4, t0:t0 + n], xq[:, :, :n], f"xo{xor_.i}", reads=[rxq])
        S.end("o")

    def phase_ffn(self, l):
        S = self.S
        S.begin()
        hT, r_hT = self.hT, self.r_hT
        last = (l == 1)
        src = self.xa.rearrange("(c p) t -> p c t", p=P)
        dst = (self.outT if last else self.xb).rearrange("(c p) t -> p c t", p=P)
        if l == 0:
            groups = [(0, [384, 384]), (768, [384, 384]), (1536, [384, 384])]
        else:
            groups = [(0, [384, 384]), (768, [384, 384]), (1536, [512])]
        o = HT_BYTES
        xacc = self.V(o, [P, DC, 768], F32); o += 49152
        w1 = [self.V(o + i * 16384, [P, DC, 512], BF16) for i in range(2)]; o += 32768
        w2 = [self.V(o + i * 16384, [P, 4, 2048], BF16) for i in range(2)]; o += 32768
        aT = [self.V(o + i * 4096, [P, 4, 512], BF16) for i in range(2)]; o += 8192
        rt = [self.V(o + i * 2048, [P, 512], F32) for i in range(2)]; o += 4096
        assert o <= ARENA_BYTES
        r_x, r_sqf = Res("xacc"), Res("sqf")
        w1r, w2r, ar, rr = Ring(w1, "w1"), Ring(w2, "w2"), Ring(aT, "aT"), Ring(rt, "rt")
        uring = Ring([self.bank(i) for i in range(3)], "U")
        yring = Ring([self.bank(3 + i) for i in range(4)], "Y")
        w1v = self.wb[f"w_ff1_{l}"].ap().rearrange("(k p) n -> p k n", p=P)
        w2v = self.wb[f"w_ff2_{l}"].ap().rearrange("(f p) n -> p f n", p=P)
        e1, e2 = self.wext(f"w_ff1_{l}"), self.wext(f"w_ff2_{l}")
        steps = [(gi, fs) for gi in range(len(groups)) for fs in range(16)]

        def wload(fs):
            b1, r1 = w1r.next()
            self.dma("sp", b1, w1v[:, :, fs * 512:(fs + 1) * 512], f"w1{w1r.i}", writes=[r1], ext=e1)
            b2, r2 = w2r.next()
            self.dma("sp", b2, w2v[:, fs * 4:(fs + 1) * 4, :], f"w2{w2r.i}", writes=[r2], ext=e2)
            return b1, r1, b2, r2
        nxt = wload(0)
        for si, (gi, fs) in enumerate(steps):
            g0, subs = groups[gi]
            ng = sum(subs)
            if fs == 0:
                self.dma("sp", xacc[:, :, :ng], src[:, :, g0:g0 + ng], "xacc", writes=[r_x])
            b1, r1, b2, r2 = nxt
            nxt = wload(steps[si + 1][1]) if si + 1 < len(steps) else None
            if l == 0 and si == 3:
                self.prologue(["w_ff1_1"], after=[r_x])
            if l == 0 and si == 19:
                self.prologue(["w_ff2_1"], after=[r_x])
            so = 0
            for n in subs:
                t0 = g0 + so
                ab, rab = ar.next()
                for fc in range(4):
                    pb, rpb = uring.next()
                    self.mmgroup(pb[:, :n], [(b1[:, k, fc * P:(fc + 1) * P], hT[:, k, t0:t0 + n]) for k in range(DC)],
                                 [r1, r_hT], [rpb])
                    rb_, rrb = rr.next()
                    S.op("act", lambda e, pb=pb, rb_=rb_, n=n: e.activation(rb_[:, :n], pb[:, :n], AF.Relu), reads=[rpb], writes=[rrb])
                    S.op("dve", lambda e, rb_=rb_, ab=ab, fc=fc, n=n: e.tensor_tensor(ab[:, fc, :n], rb_[:, :n], rb_[:, :n], ALU.mult),
                         reads=[rrb], writes=[rab])
                for m in range(DC):
                    pb, rpb = yring.next()
                    self.mmgroup(pb[:, :n], [(b2[:, fc, m * P:(m + 1) * P], ab[:, fc, :n]) for fc in range(4)], [r2, rab], [rpb])
                    segs = []
                    if t0 + n <= NL:
                        segs.append((0, n, 0))
                    elif t0 >= NL:
                        segs.append((0, n, 1))
                    else:
                        segs.append((0, NL - t0, 0))
                        segs.append((NL - t0, n, 1))
                    for (a, b, j) in segs:
                        g2 = self.modT[:, l, 5 * DC + m, j:j + 1]
                        S.op("dve", lambda e, pb=pb, m=m, a=a, b=b, g2=g2, so=so: e.scalar_tensor_tensor(
                            xacc[:, m, so + a:so + b], pb[:, a:b], g2, xacc[:, m, so + a:so + b], ALU.mult, ALU.add),
                            reads=[rpb, self.r_const], writes=[r_x])
                so += n
            if fs == 15:
                if last:
                    for so in range(0, ng, 256):
                        n = 256
                        sqv = hT[:, :, NL:NL + n]
                        S.op("act", lambda e, so=so, n=n, sqv=sqv: e.activation(sqv, xacc[:, :, so:so + n], AF.Square),
                             reads=[r_x], writes=[r_sqf])
                        sd, rsd = rr.next()
                        pb, rpb = uring.next()
                        self.rms_core(sqv, r_sqf, n, DC, self.ones_d, pb, rpb, sd, rsd)
                        for c in range(DC):
                            S.op("dve", lambda e, c=c, so=so, n=n, sd=sd: e.scalar_tensor_tensor(
                                xacc[:, c, so:so + n], xacc[:, c, so:so + n], self.fng_s[:, c:c + 1], sd[:, :n], ALU.mult, ALU.mult),
                                reads=[rsd, self.r_const], writes=[r_x])
                self.dma("sp", dst[:, :, g0:g0 + ng], xacc[:, :, :ng], "xst", reads=[r_x])
        S.end("ffn")

    def phase_p1(self):
        S = self.S
        S.begin()
        hT, r_hT = self.hT, self.r_hT
        self.KTd = self.V(TOP1, [P, 4, NT], BF16)
        self.V1a = self.V(TOP1 + 18432, [P, 18, 4, P], BF16)
        self.r_KTd, self.r_V1a = Res("KTd"), Res("V1a")
        o = HT_BYTES
        slabs = [self.V(o + i * 16384, [P, DC, 512], BF16) for i in range(4)]; o += 65536
        cosT, sinT, r_rope = self.load_rope(o, S); o += 16384
        t1 = [self.V(o + i * 2048, [P, 512], F32) for i in range(2)]; o += 4096
        qo = [self.V(o + i * 1024, [P, 512], BF16) for i in range(2)]; o += 2048
        assert o <= TOP1
        ring = Ring(slabs, "slab")
        tring, qring = Ring(t1, "t1"), Ring(qo, "qo")
        pring = Ring([self.bank(i) for i in range(6)], "pp1")
        wv = self.wb["w_in1"].ap().rearrange("(k p) n -> p k n", p=P)
        ext = self.wext("w_in1")
        S.op("dve", lambda e: e.memset(self.V1a[:, :, :, 64:128], 1.0), writes=[self.r_V1a])
        for s in range(5):
            c_main = s * 512 if s < 4 else 4096
            c_swap = 2048 + s * 512 if s < 4 else 4608
            A, rA = ring.next()
            self.dma("sp", A, wv[:, :, c_main:c_main + 512], f"slab{ring.i}", writes=[rA], ext=ext)
            B, rB = ring.next()
            self.dma("sp", B, wv[:, :, c_swap:c_swap + 512], f"slab{ring.i}", writes=[rB], ext=ext)
            for (t0, n, j) in (TILES[:4] if s < 4 else TILES):
                for mm in range(4):
                    pm, rpm = pring.next()
                    self.mmgroup(pm[:, :n], [(A[:, k, mm * P:(mm + 1) * P], hT[:, k, t0:t0 + n]) for k in range(DC)], [rA, r_hT], [rpm])
                    if j == 1:
                        S.op("act", lambda e, pm=pm, mm=mm, t0=t0, n=n: e.activation(self.KTd[:, mm, t0:t0 + n], pm[:, :n], AF.Copy),
                             reads=[rpm], writes=[self.r_KTd])
                        continue
                    pw, rpw = pring.next()
                    self.mmgroup(pw[:, :n], [(B[:, k, mm * P:(mm + 1) * P], hT[:, k, t0:t0 + n]) for k in range(DC)], [rB, r_hT], [rpw])
                    ta, rta = tring.next()
                    S.op("dve", lambda e, pm=pm, ta=ta, t0=t0, n=n: e.tensor_tensor(ta[:, :n], pm[:, :n], cosT[:, t0:t0 + n], ALU.mult),
                         reads=[rpm, r_rope], writes=[rta])
                    tb, rtb = tring.next()
                    S.op("dve", lambda e, pw=pw, tb=tb, t0=t0, n=n: e.tensor_tensor(tb[:, :n], pw[:, :n], sinT[:, t0:t0 + n], ALU.mult),
                         reads=[rpw, r_rope], writes=[rtb])
                    if s < 4:
                        qb, rqb = qring.next()
                        S.op("dve", lambda e, ta=ta, tb=tb, qb=qb, n=n: e.tensor_tensor(qb[:, :n], ta[:, :n], tb[:, :n], ALU.add),
                             reads=[rta, rtb], writes=[rqb])
                        self.dma("sp", self.q1_d[:, 4 * s + mm, t0:t0 + n], qb[:, :n], f"qo{qring.i}", reads=[rqb])
                    else:
                        S.op("dve", lambda e, ta=ta, tb=tb, mm=mm, t0=t0, n=n: e.tensor_tensor(self.KTd[:, mm, t0:t0 + n], ta[:, :n], tb[:, :n], ALU.add),
                             reads=[rta, rtb], writes=[self.r_KTd])
        A, rA = ring.next()
        self.dma("sp", A[:, :, :256], wv[:, :, 5120:5376], f"slab{ring.i}", writes=[rA], ext=ext)
        for ch in range(18):
            pm, rpm = pring.next()
            self.mmgroup(pm[:, :256], [(hT[:, k, ch * P:(ch + 1) * P], A[:, k, :256]) for k in range(DC)], [rA, r_hT], [rpm])
            S.op("act", lambda e, pm=pm, ch=ch: e.activation(self.V1a[:, ch, :, 0:64], pm[:, :256].rearrange("p (g d) -> p g d", g=4), AF.Copy),
                 reads=[rpm], writes=[self.r_V1a])
        if "p1" in self.debug:
            self.dump("KTd", self.KTd, [P, 4, NT], BF16, self.r_KTd)
            self.dump("V1a", self.V1a, [P, 18, 4, P], BF16, self.r_V1a)
        S.end("p1")

    def phase_att1(self):
        S = self.S
        S.begin()
        self.attT1 = self.V(0, [P, DC, NL], BF16)
        self.r_attT1 = Res("attT1")
        o = 65536
        mkf = self.V(o, [P, 2, 512], F32); o += 4096
        mk = self.V(o, [P, 2, 512], BF16); o += 2048
        idf = self.V(o, [P, P], F32); o += 512
        idb = self.V(o, [P, P], BF16); o += 256
        es = self.V(o, [P, 32], F32); o += 128
        qb = [self.V(o + i * 4096, [P, DC, P], BF16) for i in range(2)]; o += 8192
        PT = [self.V(o + i * 1024, [P, 512], BF16) for i in range(4)]; o += 4096
        rl = [self.V(o + i * 2048, [P, 512], F32) for i in range(2)]; o += 4096
        r_c = Res("c1")
        self.dma("sp", mkf, self.masks, "mkf", writes=[r_c])
        self.dma("sp", idf, self.ident, "idf", writes=[r_c])
        self.dma("sp", es, self.sinks.partition_broadcast(P)[:, 0, :], "es", writes=[r_c])
        S.op("act", lambda e: e.activation(mk, mkf, AF.Copy), reads=[r_c], writes=[r_c])
        S.op("act", lambda e: e.activation(idb, idf, AF.Copy), reads=[r_c], writes=[r_c])
        S.op("act", lambda e: e.activation(es, es, AF.Exp), reads=[r_c], writes=[r_c])
        qring, ptring, rlring = Ring(qb, "qb"), Ring(PT, "PT"), Ring(rl, "rl")
        sring = Ring([self.bank(i) for i in range(4)], "S1")
        oring = Ring([self.bank(4 + i) for i in range(3)], "O1")
        KTd, V1a = self.KTd, self.V1a
        for i in range(16):
            q, rq = qring.next()
            self.dma("sp", q, self.q1_d[:, :, i * P:(i + 1) * P], f"qb{qring.i}", writes=[rq])
            keys = []
            for jj in (i - 1, i, i + 1):
                if 0 <= jj < 16:
                    keys.append((jj, None if jj == i else (0 if jj == i - 1 else 1)))
            keys += [(16, None), (17, None)]
            for g in range(4):
                for half in range(2):
                    lo, hi = half * 64, half * 64 + 64
                    ob, rob = oring.next()

                    def s_op(kc, mid):
                        pb, rpb = sring.next()

                        def f(e, pb=pb, kc=kc, mid=mid, lo=lo, hi=hi, g=g, q=q):
                            ins = e.matmul(pb[:, :].rearrange("p (a b) -> p a b", b=P), KTd[lo:hi, g, kc * P:(kc + 1) * P],
                                           q[lo:hi, 4 * g:4 * g + 4, :], start=True, stop=(mid is None))
                            if mid is not None:
                                ins = e.matmul(pb[:, :], idb, mk[:, mid, :], start=False, stop=True)
                            return ins
                        S.op("pe", f, reads=[self.r_KTd, rq, r_c], writes=[rpb])
                        pt, rpt = ptring.next()
                        S.op("act", lambda e, pb=pb, pt=pt: e.activation(pt, pb, AF.Exp, scale=SWA_SCALE), reads=[rpb], writes=[rpt])
                        return pt, rpt
                    cur = s_op(*keys[0])
                    for ki, (kc, mid) in enumerate(keys):
                        nxt = s_op(*keys[ki + 1]) if ki + 1 < len(keys) else None
                        pt, rpt = cur
                        S.op("pe", lambda e, pt=pt, kc=kc, ki=ki, ob=ob, lastk=(ki == len(keys) - 1), g=g: e.matmul(
                            ob, V1a[:, kc, g, :], pt, start=(ki == 0), stop=lastk), reads=[rpt, self.r_V1a], writes=[rob])
                        cur = nxt
                    rb_, rrl = rlring.next()
                    h0 = 8 * g + half
                    esb = bcast_ap(es[64:128, h0:h0 + 1], [[2, 4], [0, P]])
                    S.op("dve", lambda e, rb_=rb_, ob=ob, esb=esb: e.tensor_tensor(
                        rb_[64:128, :].rearrange("p (a b) -> p a b", b=P), ob[64:128, :].rearrange("p (a b) -> p a b", b=P), esb, ALU.add),
                        reads=[rob, r_c], writes=[rrl])
                    S.op("dve", lambda e, rb_=rb_: e.reciprocal(rb_[64:128, :], rb_[64:128, :]), reads=[rrl], writes=[rrl])
                    S.op("dve", lambda e, rb_=rb_, ob=ob, i=i, lo=lo, hi=hi, g=g: e.tensor_tensor(
                        self.attT1[lo:hi, 4 * g:4 * g + 4, i * P:(i + 1) * P], ob[0:64, :].rearrange("p (a b) -> p a b", b=P),
                        rb_[64:128, :].rearrange("p (a b) -> p a b", b=P), ALU.mult), reads=[rob, rrl], writes=[self.r_attT1])
        if "att1" in self.debug:
            self.dump("attT1", self.attT1, [P, DC, NL], BF16, self.r_attT1)
        S.end("att1")


def _pc(v):
    return np.ascontiguousarray(v.reshape(-1, P).T)


def rope_tables():
    t = np.arange(NL)
    row = (t // 64).astype(np.float32)
    col = (t % 64).astype(np.float32)
    inv = (10000.0 ** (-np.arange(16, dtype=np.float32) / 16)).astype(np.float32)
    ang = np.concatenate([row[:, None] * inv, col[:, None] * inv], axis=-1)
    cos = np.cos(ang).astype(np.float32).T
    sin = np.sin(ang).astype(np.float32).T
    c4 = np.concatenate([cos, cos, cos, cos], 0)
    s4 = np.concatenate([-sin, sin, -sin, sin], 0)
    return np.ascontiguousarray(c4), np.ascontiguousarray(s4)


def swap_halves(w, hd):
    sh = w.shape
    w = w.reshape(sh[:-1] + (sh[-1] // hd, 2, hd // 2))
    return np.ascontiguousarray(w[..., ::-1, :]).reshape(sh)


def make_shared(inp):
    f = np.float32
    sh = {}
    sh["n1g"] = np.ascontiguousarray(np.stack([_pc(inp["norm1_g"][l]) for l in range(2)], 1), f)
    sh["n2g"] = np.ascontiguousarray(np.stack([_pc(inp["norm2_g"][l]) for l in range(2)], 1), f)
    sh["fng"] = _pc(inp["final_norm_g"]).astype(f)
    sh["bmod"] = np.ascontiguousarray(np.stack([_pc(inp["b_mod"][l]) for l in range(2)], 1), f)
    call = np.concatenate([inp["c"], inp["c_ctx"][None], np.zeros((1, D), f)], 0)
    sh["cT"] = np.ascontiguousarray(call.reshape(10, DC, P).transpose(2, 1, 0), f)
    sh["qng"] = _pc(inp["mla_q_norm_g"][0]).astype(f)
    sh["kvng"] = _pc(inp["mla_kv_norm_g"][0]).astype(f)
    sh["w_spT"] = np.ascontiguousarray(inp["gmlp_w_sp"][0].transpose(2, 0, 1), f)
    sh["b_sp"] = np.ascontiguousarray(inp["gmlp_b_sp"][0].reshape(1, 1024), f)
    sh["sinks"] = np.ascontiguousarray(inp["swa_sinks"].reshape(1, 32), f)
    c4, s4 = rope_tables()
    sh["ropec"], sh["ropes"] = c4, s4
    kl = np.arange(P)[:, None]
    ql = np.arange(P)[None, :]
    m_prev = np.where(kl >= ql, 0.0, NEG)
    m_next = np.where(kl <= ql, 0.0, NEG)
    sh["masks"] = np.ascontiguousarray(np.stack([np.tile(m_prev, (1, 4)), np.tile(m_next, (1, 4))], 1), f)
    sh["ident"] = np.eye(P, dtype=f)
    return sh


def make_weights(inp):
    f = np.float32
    W = {}
    wi = inp["even_w_in"][0]
    cq, ckv, kr, gm = wi[:, :512], wi[:, 512:1024], wi[:, 1024:1088], wi[:, 1088:]
    W["w_in0"] = np.concatenate([cq, ckv, gm, kr, swap_halves(kr, 64)], 1)
    wq = inp["mla_w_uq"][0].reshape(512, 8, 192)
    qn = wq[:, :, :128].reshape(512, 1024)
    qr = wq[:, :, 128:]
    W["w_q"] = np.concatenate([qn, np.concatenate([qr, swap_halves(qr, 64)], 2).reshape(512, 1024)], 1)
    wkv = inp["mla_w_ukv"][0].reshape(512, 8, 256)
    W["w_kv"] = np.concatenate([wkv[:, :, :128].reshape(512, 1024), wkv[:, :, 128:].reshape(512, 1024)], 1)
    W["w_out0"] = inp["even_w_out"][0]
    w1 = inp["odd_w_in"][0]
    q, k, v = w1[:, :2048], w1[:, 2048:2304], w1[:, 2304:]
    kd = np.repeat(k.reshape(D, 4, 1, 64), 2, axis=2).reshape(D, 512)
    W["w_in1"] = np.concatenate([q, swap_halves(q, 64), kd, swap_halves(kd, 64), v], 1)
    W["w_out1"] = inp["odd_w_out"][0]
    for l in range(2):
        W[f"w_ff1_{l}"] = inp["w_ff1"][l]
        W[f"w_ff2_{l}"] = inp["w_ff2"][l]
    return {k: np.ascontiguousarray(v, f) for k, v in W.items()}


def make_core_inputs(inp, b, r, nr, shared, W):
    m = dict(shared)
    m["xT"] = np.ascontiguousarray(np.concatenate([inp["x"][b], inp["ctx"][b]], 0).T)
    sel = np.zeros((P, 8), np.float32)
    sel[:, b] = 1.0
    m["sel"] = sel
    cpr = 6 * D // nr
    m["w_mod"] = np.ascontiguousarray(inp["w_mod"][:, :, r * cpr:(r + 1) * cpr])
    for (name, rows, cols) in WSPEC:
        rp = rows // nr
        m["ws_" + name] = np.ascontiguousarray(W[name][r * rp:(r + 1) * rp])
    return m


_PROG = None
NSHARD = 1


def kernel(**inputs):
    global _PROG
    inp = {k: np.asarray(v) for k, v in inputs.items()}
    if _PROG is None:
        _PROG = Prog(ncores=NSHARD)
        _PROG.build()
    shared = make_shared(inp)
    W = make_weights(inp)
    if NSHARD == 1:
        in_maps = [make_core_inputs(inp, b, 0, 1, shared, W) for b in range(8)]
    else:
        in_maps = [make_core_inputs(inp, b, b, 8, shared, W) for b in range(8)]
    res = run_bass_kernel_spmd(_PROG.nc, in_maps, core_ids=list(range(8)))
    out = np.stack([np.ascontiguousarray(r["outT"].T) for r in res.results], 0)
    return out.astype(np.float32)
```

```python
import math
from contextlib import ExitStack

import numpy as np
import concourse.bass as bass
import concourse.mybir as mybir
from concourse.bass_utils import run_bass_kernel_spmd

F32 = mybir.dt.float32
BF16 = mybir.dt.bfloat16
AF = mybir.ActivationFunctionType
ALU = mybir.AluOpType
AX = mybir.AxisListType

P = 128
D = 2048
DC = 16
NL = 2048
NX = 256
NT = NL + NX
DFF = 8192
EPS = 1e-6
MLA_SCALE = 1.0 / math.sqrt(192.0)
SWA_SCALE = 0.125
NEG = -30000.0
TILES = [(0, 512, 0), (512, 512, 0), (1024, 512, 0), (1536, 512, 0), (2048, 256, 1)]


class Res:
    __slots__ = ("name", "lw", "rd")

    ALL = []

    def __init__(self, name):
        self.name = name
        self.lw = None
        self.rd = []
        Res.ALL.append(self)


class _Op:
    __slots__ = ("eng", "fn", "deps", "key", "val", "signal", "idx", "amt", "ext")


class Sched:
    ENGS = ("pe", "act", "dve", "pool", "sp")

    def __init__(self, nc, stack):
        self.nc = nc
        self.stack = stack
        self.sems = {}
        self.ops = []
        self.phase_keys = set()
        self.nphase = 0

    def _sem(self, name):
        if name not in self.sems:
            h = self.stack.enter_context(self.nc.semaphore("s_" + name))
            self.sems[name] = [h, 0]
        return self.sems[name]

    def begin(self):
        self.ops = []
        self.phase_keys = set()
        for r in Res.ALL:
            r.lw = None
            r.rd = []

    def op(self, eng, fn, reads=(), writes=(), key=None, amt=16, ext=(), nodrain=False, peek=()):
        o = _Op()
        o.amt = amt
        o.ext = list(ext)
        o.eng = eng
        o.fn = fn
        o.key = key
        o.signal = False
        o.val = None
        o.idx = len(self.ops)
        deps = set()
        for r in reads:
            if r.lw is not None:
                deps.add(r.lw)
        for r in peek:
            if r.lw is not None:
                deps.add(r.lw)
        for w in writes:
            if w.lw is not None:
                deps.add(w.lw)
            deps.update(w.rd)
        for r in reads:
            r.rd.append(o.idx)
        for w in writes:
            w.lw = o.idx
            w.rd = []
        deps.discard(o.idx)
        o.deps = sorted(deps)
        if key is not None:
            s = self._sem("d_" + key)
            s[1] += amt
            o.val = s[1]
            if not nodrain:
                self.phase_keys.add(key)
        self.ops.append(o)
        return o

    def end(self, name=None):
        ops = self.ops
        for o in ops:
            for d in o.deps:
                do = ops[d]
                if do.key is not None:
                    continue
                if do.eng == "pe" and o.eng == "pe" and o.key is None:
                    continue
                do.signal = True
        for o in ops:
            if o.key is None and o.signal:
                s = self._sem("e_" + o.eng)
                s[1] += 1
                o.val = s[1]
        streams = {e: [] for e in self.ENGS}
        known = {e: {} for e in self.ENGS}
        for o in ops:
            waits = {}
            for d in o.deps:
                do = ops[d]
                if do.key is not None:
                    sn = "d_" + do.key
                else:
                    if do.eng == "pe" and o.eng == "pe" and o.key is None:
                        continue
                    sn = "e_" + do.eng
                if do.val > waits.get(sn, 0):
                    waits[sn] = do.val
            for sn, v in o.ext:
                if v > waits.get(sn, 0):
                    waits[sn] = v
            wl = []
            for sn, v in waits.items():
                if known[o.eng].get(sn, 0) >= v:
                    continue
                known[o.eng][sn] = v
                wl.append((self.sems[sn][0], v))
            streams[o.eng].append((o, wl))
        fin = []
        for k in sorted(self.phase_keys):
            s = self.sems["d_" + k]
            fin.append((s[0], s[1]))

        def run(eng_name, e):
            for o, wl in streams[eng_name]:
                for h, v in wl:
                    e.wait_ge(h, v)
                ins = o.fn(e)
                if o.key is not None:
                    ins.then_inc(self.sems["d_" + o.key][0], o.amt)
                elif o.signal:
                    ins.then_inc(self.sems["e_" + o.eng][0], 1)
            if eng_name == "sp":
                for h, v in fin:
                    e.wait_ge(h, v)

        self.nphase += 1
        with self.nc.Block() as block:
            if streams["pe"]:
                block.tensor(lambda e: run("pe", e))
            if streams["act"]:
                block.scalar(lambda e: run("act", e))
            if streams["dve"]:
                block.vector(lambda e: run("dve", e))
            if streams["pool"]:
                block.gpsimd(lambda e: run("pool", e))
            block.sync(lambda e: run("sp", e))
        self.ops = []


class Ring:
    def __init__(self, views, name):
        self.views = views
        self.res = [Res(f"{name}{i}") for i in range(len(views))]
        self.i = -1

    def next(self):
        self.i = (self.i + 1) % len(self.views)
        return self.views[self.i], self.res[self.i]


def bcast_ap(ap, shape_steps):
    return bass.AP(ap.tensor, ap.offset, [list(ap.ap[0])] + [list(x) for x in shape_steps])


ARENA_BYTES = 200704
HT_BYTES = DC * NT * 2
TOP0 = ARENA_BYTES - 41472
TOP1 = ARENA_BYTES - 36864

WSPEC = [("w_in0", D, 3200), ("w_q", 512, 2048), ("w_kv", 512, 2048), ("w_out0", D, D),
         ("w_ff1_0", D, DFF), ("w_ff2_0", DFF, D), ("w_in1", D, 5376), ("w_out1", D, D),
         ("w_ff1_1", D, DFF), ("w_ff2_1", DFF, D)]


class Prog:
    def __init__(self, ncores=8, debug=None, stop=None):
        self.ncores = ncores
        self.debug = debug or set()
        self.stop = stop
        self.nc = bass.Bass("TRN2", target_bir_lowering=False)
        self.gstack = ExitStack()
        self.S = Sched(self.nc, self.gstack)
        self.wready = {}

    def din(self, name, shape, dt=F32):
        return self.nc.dram_tensor(name, list(shape), dt, kind="ExternalInput").ap()

    def dout(self, name, shape, dt=F32):
        return self.nc.dram_tensor(name, list(shape), dt, kind="ExternalOutput").ap()

    def dscr(self, name, shape, dt):
        if "scr" in self.debug:
            return self.nc.dram_tensor(name, list(shape), dt, kind="ExternalOutput").ap()
        return self.nc.dram_tensor(name, list(shape), dt).ap()

    def V(self, off, shape, dt):
        n = 1
        for x in shape[1:]:
            n *= x
        esz = 4 if dt == F32 else 2
        assert off % 4 == 0 and off + n * esz <= ARENA_BYTES, (off, shape)
        ap = self.arena[:, off // 2: off // 2 + n * esz // 2]
        if dt == F32:
            ap = ap.bitcast(F32)
        if len(shape) == 3:
            ap = ap.rearrange("p (a b) -> p a b", a=shape[1])
        elif len(shape) == 4:
            ap = ap.rearrange("p (a b c) -> p a b c", a=shape[1], b=shape[2])
        return ap

    def bank(self, b):
        return self.psum[:, b * 512:(b + 1) * 512]

    def dma(self, q, out, in_, key, reads=(), writes=(), ext=()):
        self.S.op(q, lambda e, out=out, in_=in_: e.dma_start(out=out, in_=in_), reads=reads, writes=writes,
                  key=key, ext=ext)

    def wext(self, name):
        return [self.wready[name]]

    def dump(self, name, src_ap, shape, dt, res):
        d = self.nc.dram_tensor("dbg_" + name, list(shape), dt, kind="ExternalOutput").ap()
        self.dma("sp", d, src_ap, "dbg_" + name, reads=[res])

    def mmgroup(self, out, pairs, reads, writes):
        def f(e):
            n = len(pairs)
            for i, (l, r) in enumerate(pairs):
                ins = e.matmul(out, l, r, start=(i == 0), stop=(i == n - 1))
            return ins
        self.S.op("pe", f, reads=reads, writes=writes)

    def mod_vec(self, l, kind, j):
        return self.modT[:, l, kind * DC:(kind + 1) * DC, j]

    def build(self):
        nc, S, g = self.nc, self.S, self.gstack
        nr = self.ncores
        self.xT = self.din("xT", [D, NT])
        self.cT = self.din("cT", [P, DC, 10])
        self.sel = self.din("sel", [P, 8])
        self.n1g = self.din("n1g", [P, 2, DC])
        self.n2g = self.din("n2g", [P, 2, DC])
        self.fng = self.din("fng", [P, DC])
        self.bmod = self.din("bmod", [P, 2, 96])
        self.w_mod = self.din("w_mod", [2, D, 6 * D // nr])
        self.qng = self.din("qng", [P, 4])
        self.kvng = self.din("kvng", [P, 4])
        self.w_spT = self.din("w_spT", [P, 8, P])
        self.b_sp = self.din("b_sp", [1, 8 * P])
        self.sinks = self.din("sinks", [1, 32])
        self.ropec = self.din("ropec", [P, NL])
        self.ropes = self.din("ropes", [P, NL])
        self.masks = self.din("masks", [P, 2, 512])
        self.ident = self.din("ident", [P, P])
        self.ws = {}
        self.wb = {}
        self.wsb = {}
        for (name, r, c) in WSPEC:
            self.ws[name] = self.din("ws_" + name, [r // nr, c])
            self.wb[name] = nc.dram_tensor("wb_" + name, [r, c], BF16)
            if nr > 1:
                self.wsb[name] = nc.dram_tensor("wsb_" + name, [r // nr, c], BF16)
        self.outT = self.dout("outT", [D, NL])
        self.xa = self.dscr("xa", [D, NT], F32)
        self.xb = self.dscr("xb", [D, NT], F32)
        self.u_d = self.dscr("u_d", [P, 8, NT], BF16)
        self.vn_d = self.dscr("vn_d", [18, P, 1024], BF16)
        self.q1_d = self.dscr("q1_d", [P, 16, NL], BF16)
        self.mcpr = 96 // nr
        self.mod_sh = nc.dram_tensor("mod_sh", [P, 2 * self.mcpr * 10], F32)
        self.mod_all = nc.dram_tensor("mod_all", [nr * P, 2 * self.mcpr * 10], F32)

        sbt = lambda name, shape, dt: g.enter_context(nc.sbuf_tensor(name, list(shape), dt))
        self.modT = sbt("modT", [P, 2, 96, 2], F32)
        self.A1 = sbt("A1", [P, 2, DC, 2], F32)
        self.A2 = sbt("A2", [P, 2, DC, 2], F32)
        self.fng_s = sbt("fng_s", [P, DC], F32)
        self.qng_s = sbt("qng_s", [P, 4], F32)
        self.kvng_s = sbt("kvng_s", [P, 4], F32)
        self.ones_d = sbt("ones_d", [P, P], BF16)
        self.ones_q = sbt("ones_q", [P, P], BF16)
        self.ones_1 = sbt("ones_1", [P, P], BF16)
        self.eps_t = sbt("eps_t", [P, 1], F32)
        self.sbf_g = sbt("sbf_g", [P, DC, 10], BF16)
        self.sel_g = sbt("sel_g", [P, 8], F32)
        self.bm_g = sbt("bm_g", [P, 2, 96], F32)
        self.n1s_g = sbt("n1s_g", [P, 2, DC], F32)
        self.n2s_g = sbt("n2s_g", [P, 2, DC], F32)
        self.Rt_g = sbt("Rt_g", [P, 96, 10], F32)
        self.tmp_g = sbt("tmp_g", [P, 32, 8], F32)
        self.r_sbf, self.r_small, self.r_R, self.r_mod, self.r_tmpg = (Res(x) for x in ("sbf", "small", "R", "modT", "tmpg"))
        self.arena = sbt("arena", [P, ARENA_BYTES // 2], BF16)
        self.psum = g.enter_context(nc.psum_tensor("psum", [P, 8 * 512], F32))
        self.r_const = Res("const")

        phases = [
            ("mod", lambda: self.phase_mod()),
            ("n1_0", lambda: self.phase_norm(self.xT, 0, self.A1, 0, TILES)),
            ("p0", lambda: self.phase_p0()),
            ("att0", lambda: self.phase_att0()),
            ("g0", lambda: self.phase_g0()),
            ("o0", lambda: self.phase_o(0)),
            ("n2_0", lambda: self.phase_norm(self.xa, 0, self.A2, 3, TILES)),
            ("f0", lambda: self.phase_ffn(0)),
            ("n1_1", lambda: self.phase_norm(self.xb, 1, self.A1, 0, TILES)),
            ("p1", lambda: self.phase_p1()),
            ("att1", lambda: self.phase_att1()),
            ("o1", lambda: self.phase_o(1)),
            ("n2_1", lambda: self.phase_norm(self.xa, 1, self.A2, 3, TILES[:4])),
            ("f1", lambda: self.phase_ffn(1)),
        ]
        for name, fn in phases:
            fn()
            if self.stop == name:
                break
        self.gstack.close()

    def phase_mod(self):
        S = self.S
        assert self.ncores == 1
        S.begin()
        cin = self.V(0, [P, DC, 10], F32)
        slab = [self.V(4096 + i * 12288, [P, DC, 384], BF16) for i in range(2)]
        r_cin = Res("cin")
        S.op("dve", lambda e: e.memset(self.ones_d[:], 1.0 / D), writes=[self.r_const])
        S.op("dve", lambda e: e.memset(self.ones_q[:], 1.0 / 512), writes=[self.r_const])
        S.op("dve", lambda e: e.memset(self.ones_1[:], 1.0), writes=[self.r_const])
        S.op("dve", lambda e: e.memset(self.eps_t[:], EPS), writes=[self.r_const])
        self.dma("sp", cin, self.cT, "cin", writes=[r_cin])
        self.dma("sp", self.n1s_g[:], self.n1g, "sm1", writes=[self.r_small])
        self.dma("sp", self.n2s_g[:], self.n2g, "sm2", writes=[self.r_small])
        self.dma("sp", self.bm_g[:], self.bmod, "sm3", writes=[self.r_small])
        self.dma("sp", self.sel_g[:], self.sel, "sm4", writes=[self.r_small])
        self.dma("sp", self.fng_s[:], self.fng, "sm5", writes=[self.r_const])
        self.dma("sp", self.qng_s[:], self.qng, "sm6", writes=[self.r_const])
        self.dma("sp", self.kvng_s[:], self.kvng, "sm7", writes=[self.r_const])
        S.op("act", lambda e: e.activation(self.sbf_g[:], cin, AF.Silu), reads=[r_cin], writes=[self.r_sbf])
        self.prologue(["w_in0", "w_q", "w_kv"])
        ring = Ring(slab, "mslab")
        pring = Ring([self.bank(0)[:, 0:30], self.bank(1)[:, 0:30]], "pmod")
        jobs = self.mod_jobs(0, 0, 11, ring, pring)
        for j in jobs:
            j()
        self.mod_finish(0, 0, 32)
        if "mod" in self.debug:
            self.dump("modT", self.modT[:], [P, 2, 96, 2], F32, self.r_mod)
        S.end("mod")

    def mod_jobs(self, l, s0, s1, ring, pring):
        S = self.S
        wv = self.w_mod[l].rearrange("(k p) n -> p k n", p=P)
        st = {}

        def load(s):
            buf, rb = ring.next()
            st[s] = (buf, rb)
            self.dma("pool", buf, wv[:, :, s * 384:(s + 1) * 384], f"{ring.res[ring.i].name}", writes=[rb])

        def comp(s):
            buf, rb = st.pop(s)
            pb, rpb = pring.next()
            for mm in range(3):
                self.mmgroup(pb[:, mm * 10:(mm + 1) * 10],
                             [(buf[:, k, mm * P:(mm + 1) * P], self.sbf_g[:, k, :]) for k in range(DC)],
                             [rb, self.r_sbf], [rpb])
            S.op("act", lambda e, pb=pb, s=s: e.activation(
                self.Rt_g[:, s * 3:(s + 1) * 3, :], pb.rearrange("p (a b) -> p a b", b=10), AF.Copy),
                reads=[rpb], writes=[self.r_R])
        jobs = []
        ss = list(range(s0, s1))
        jobs.append(lambda: [load(x) for x in ss[:2]])
        for i, s in enumerate(ss):
            jobs.append(lambda s=s, i=i: (comp(s), load(ss[i + 2]) if i + 2 < len(ss) else None))
        return jobs

    def mod_finish(self, l, m0, m1):
        S = self.S
        for c0 in range(m0, m1, 32):
            c1 = min(c0 + 32, m1)
            n = c1 - c0
            selb = bcast_ap(self.sel_g[:], [[0, n], [1, 8]])
            S.op("dve", lambda e, c0=c0, c1=c1, n=n, selb=selb: e.tensor_tensor(
                self.tmp_g[:, 0:n, :], self.Rt_g[:, c0:c1, 0:8], selb, ALU.mult),
                reads=[self.r_R, self.r_small], writes=[self.r_tmpg])
            S.op("dve", lambda e, c0=c0, c1=c1, n=n: e.tensor_reduce(self.modT[:, l, c0:c1, 0], self.tmp_g[:, 0:n, :], AX.X, ALU.add),
                 reads=[self.r_tmpg], writes=[self.r_mod])
            S.op("act", lambda e, c0=c0, c1=c1: e.activation(self.modT[:, l, c0:c1, 1], self.Rt_g[:, c0:c1, 8], AF.Copy),
                 reads=[self.r_R], writes=[self.r_mod])
            bv = bcast_ap(self.bm_g[:, l, c0:c1], [[1, n], [0, 2]])
            S.op("dve", lambda e, c0=c0, c1=c1, bv=bv: e.tensor_tensor(self.modT[:, l, c0:c1, :], self.modT[:, l, c0:c1, :], bv, ALU.add),
                 reads=[self.r_mod, self.r_small], writes=[self.r_mod])
        for (A, ns, kind) in ((self.A1, self.n1s_g, 1), (self.A2, self.n2s_g, 4)):
            if m0 <= kind * DC and (kind + 1) * DC <= m1:
                for j in range(2):
                    S.op("dve", lambda e, A=A, ns=ns, kind=kind, j=j: e.scalar_tensor_tensor(
                        A[:, l, :, j], self.mod_vec(l, kind, j), 1.0, ns[:, l, :], ALU.add, ALU.mult),
                        reads=[self.r_mod, self.r_small], writes=[self.r_const])

    def cast_parts(self, name, nb):
        S = self.S
        r = dict((n, rr) for (n, rr, c) in WSPEC)[name]
        rb = r // nb
        self.wready[name] = ("d_g_" + name, 16 * nb)

        def mk(i):
            def go(after=()):
                S.op("pool", lambda e: e.dma_start(
                    out=self.wb[name].ap()[i * rb:(i + 1) * rb, :], in_=self.ws[name][i * rb:(i + 1) * rb, :]),
                    peek=list(after), key="g_" + name, nodrain=True)
            return go
        return [mk(i) for i in range(nb)]

    def prologue(self, names, after=()):
        S, nr = self.S, self.ncores
        for (name, r, c) in WSPEC:
            if name not in names:
                continue
            if nr > 1:
                rs = Res("wsb_" + name)
                self.dma("pool", self.wsb[name].ap(), self.ws[name], "cast_" + name, writes=[rs])
                S.op("pool", lambda e, name=name: e.collective_compute(
                    "AllGather", ALU.bypass, replica_groups=[list(range(nr))],
                    ins=[self.wsb[name].ap().opt()], outs=[self.wb[name].ap().opt()]),
                    reads=[rs], key="g_" + name, amt=1, nodrain=True)
                self.wready[name] = ("d_g_" + name, 1)
            else:
                nb = 8
                rb = r // nb
                for i in range(nb):
                    S.op("pool", lambda e, name=name, i=i, rb=rb: e.dma_start(
                        out=self.wb[name].ap()[i * rb:(i + 1) * rb, :], in_=self.ws[name][i * rb:(i + 1) * rb, :]),
                        reads=list(after), key="g_" + name, nodrain=True)
                self.wready[name] = ("d_g_" + name, 16 * nb)

    def rms_core(self, sqv, r_sq, n, nch, ones, psb, r_ps, sd, r_sd):
        S = self.S
        self.mmgroup(psb[:, :n], [(ones[:], sqv[:, c, :n]) for c in range(nch)], [r_sq, self.r_const], [r_ps])
        S.op("act", lambda e: e.activation(sd[:, :n], psb[:, :n], AF.Sqrt, bias=self.eps_t[:, 0:1]),
             reads=[r_ps, self.r_const], writes=[r_sd])
        S.op("dve", lambda e: e.reciprocal(sd[:, :n], sd[:, :n]), reads=[r_sd], writes=[r_sd])

    def phase_norm(self, src, l, A, sh_kind, tiles):
        S = self.S
        S.begin()
        self.hT = self.V(0, [P, DC, NT], BF16)
        self.r_hT = Res("hT")
        srcv = src.rearrange("(c p) t -> p c t", p=P)
        T0 = HT_BYTES
        xbufs = [self.V(T0 + i * 32768, [P, DC, 512], F32) for i in range(2)]
        sq = self.V(T0 + 65536, [P, DC, 512], BF16)
        sd = self.V(T0 + 81920, [P, 512], F32)
        r_sq, r_ps, r_sd = Res("sq"), Res("ps"), Res("sd")
        psb = self.bank(0)
        ring = Ring(xbufs, "xbuf")
        A_l = A[:, l]
        sh_l = self.modT[:, l, sh_kind * DC:(sh_kind + 1) * DC, :]

        rxc = [[Res(f"xb{i}c{c}") for c in range(DC)] for i in range(2)]

        def one(pend):
            xs, rx, t0, n, j = pend
            rc = rxc[0] if rx is ring.res[0] else rxc[1]
            S.op("act", lambda e: e.activation(sq[:, :, :n], xs[:, :, :n], AF.Square), reads=[rx], writes=[r_sq] + rc)
            self.rms_core(sq, r_sq, n, DC, self.ones_d, psb, r_ps, sd, r_sd)
            for c in range(DC):
                S.op("dve", lambda e, c=c: e.scalar_tensor_tensor(xs[:, c, :n], xs[:, c, :n], A_l[:, c, j:j + 1],
                                                                  sd[:, :n], ALU.mult, ALU.mult),
                     reads=[r_sd, self.r_const], writes=[rc[c]])
                S.op("act", lambda e, c=c: e.activation(self.hT[:, c, t0:t0 + n], xs[:, c, :n], AF.Identity,
                                                        bias=sh_l[:, c, j:j + 1]),
                     reads=[rc[c], rx, self.r_const], writes=[self.r_hT])
        pend = None
        for (t0, n, j) in tiles:
            xs, rx = ring.next()
            self.dma("sp", xs[:, :, :n], srcv[:, :, t0:t0 + n], f"xbuf{ring.i}", writes=[rx])
            if pend is not None:
                one(pend)
            pend = (xs, rx, t0, n, j)
        one(pend)
        if "norm" in self.debug:
            self.dump("hT", self.hT, [P, DC, NT], BF16, self.r_hT)
        S.end("norm")

    def load_rope(self, off, S):
        cosT = self.V(off, [P, NL], F32)
        sinT = self.V(off + 8192, [P, NL], F32)
        r_rope = Res("rope")
        self.dma("sp", cosT, self.ropec, "ropec", writes=[r_rope])
        self.dma("sp", sinT, self.ropes, "ropes", writes=[r_rope])
        return cosT, sinT, r_rope

    def phase_p0(self):
        S = self.S
        S.begin()
        casts = iter(self.cast_parts("w_out0", 8) + self.cast_parts("w_ff1_0", 32) + self.cast_parts("w_ff2_0", 32))
        hT, r_hT = self.hT, self.r_hT
        self.cqn = self.V(TOP0, [P, 4, NT], BF16)
        self.ckvn = self.V(TOP0 + 18432, [P, 4, NT], BF16)
        self.krT = self.V(TOP0 + 36864, [P, NT], BF16)
        self.r_cqn, self.r_ckvn, self.r_krT = Res("cqn"), Res("ckvn"), Res("krT")
        T0 = HT_BYTES
        slabs = [self.V(T0 + i * 16384, [P, DC, 512], BF16) for i in range(2)]
        o = T0 + 32768
        cq = self.V(o, [P, 4, 512], F32); o += 8192
        sq4 = self.V(o, [P, 4, 512], BF16); o += 4096
        sd = self.V(o, [P, 512], F32); o += 2048
        uo = [self.V(o + i * 1024, [P, 512], BF16) for i in range(2)]; o += 2048
        vg = [self.V(o + i * 2048, [P, 512], F32) for i in range(2)]; o += 4096
        vsq = self.V(o, [P, 512], F32); o += 2048
        stt = self.V(o, [P, 24], F32); o += 96
        vno = [self.V(o + i * 1024, [P, 512], BF16) for i in range(2)]; o += 2048
        cosT, sinT, r_rope = self.load_rope(o, S); o += 16384
        t1 = self.V(o, [P, 512], F32); o += 2048
        t2 = self.V(o, [P, 512], F32); o += 2048
        assert o <= TOP0
        r_cq, r_sq4, r_sd, r_vsq, r_st, r_t1, r_t2, r_pst = (Res(x) for x in "cq sq4 sd vsq st t1 t2 pst".split())
        ring = Ring(slabs, "slab")
        pring = Ring([self.bank(i) for i in range(4)], "pp")
        uring = Ring(uo, "uo")
        vgring = Ring(vg, "vg")
        vnring = Ring(vno, "vno")
        pst = self.bank(4)
        wv = self.wb["w_in0"].ap().rearrange("(k p) n -> p k n", p=P)
        ext = self.wext("w_in0")
        for s in range(7):
            ncol = 512 if s < 6 else 128
            buf, rb = ring.next()
            self.dma("sp", buf[:, :, :ncol], wv[:, :, s * 512:s * 512 + ncol], f"slab{ring.i}", writes=[rb], ext=ext)
            for (t0, n, j) in TILES:
                for _ in range(2):
                    cj = next(casts, None)
                    if cj is not None:
                        cj(after=[pring.res[pring.i]] if pring.i >= 0 else [])
                if s < 2:
                    dst, r_dst, gn = (self.cqn, self.r_cqn, self.qng_s) if s == 0 else (self.ckvn, self.r_ckvn, self.kvng_s)
                    for mm in range(4):
                        pb, rpb = pring.next()
                        self.mmgroup(pb[:, :n], [(buf[:, k, mm * P:(mm + 1) * P], hT[:, k, t0:t0 + n]) for k in range(DC)],
                                     [rb, r_hT], [rpb])
                        S.op("act", lambda e, pb=pb, mm=mm, n=n: e.activation(cq[:, mm, :n], pb[:, :n], AF.Copy),
                             reads=[rpb], writes=[r_cq])
                        S.op("act", lambda e, pb=pb, mm=mm, n=n: e.activation(sq4[:, mm, :n], pb[:, :n], AF.Square),
                             reads=[rpb], writes=[r_sq4])
                    self.rms_core(sq4, r_sq4, n, 4, self.ones_q, pst, r_pst, sd, r_sd)
                    for mm in range(4):
                        S.op("dve", lambda e, mm=mm, dst=dst, gn=gn, t0=t0, n=n: e.scalar_tensor_tensor(
                            dst[:, mm, t0:t0 + n], cq[:, mm, :n], gn[:, mm:mm + 1], sd[:, :n], ALU.mult, ALU.mult),
                            reads=[r_cq, r_sd, self.r_const], writes=[r_dst])
                elif s < 4:
                    for mm in range(4):
                        gidx = (s - 2) * 4 + mm
                        pb, rpb = pring.next()
                        self.mmgroup(pb[:, :n], [(buf[:, k, mm * P:(mm + 1) * P], hT[:, k, t0:t0 + n]) for k in range(DC)],
                                     [rb, r_hT], [rpb])
                        ub, rub = uring.next()
                        S.op("act", lambda e, pb=pb, ub=ub, n=n: e.activation(ub[:, :n], pb[:, :n], AF.Gelu_apprx_tanh),
                             reads=[rpb], writes=[rub])
                        self.dma("sp", self.u_d[:, gidx, t0:t0 + n], ub[:, :n], f"uo{uring.i}", reads=[rub])
                elif s < 6:
                    for jj in range(n // P):
                        ch = t0 // P + jj
                        pb, rpb = pring.next()
                        self.mmgroup(pb[:, :], [(hT[:, k, ch * P:(ch + 1) * P], buf[:, k, :]) for k in range(DC)],
                                     [rb, r_hT], [rpb])
                        vb, rvb = vgring.next()
                        S.op("act", lambda e, pb=pb, vb=vb: e.activation(vb, pb, AF.Gelu_apprx_tanh), reads=[rpb], writes=[rvb])
                        S.op("act", lambda e, vb=vb: e.activation(vsq, vb, AF.Square), reads=[rvb], writes=[r_vsq])
                        S.op("dve", lambda e, vb=vb: e.tensor_reduce(stt[:, 0:4], vb.rearrange("p (g c) -> p g c", g=4), AX.X, ALU.add),
                             reads=[rvb], writes=[r_st])
                        S.op("dve", lambda e: e.tensor_reduce(stt[:, 4:8], vsq.rearrange("p (g c) -> p g c", g=4), AX.X, ALU.add),
                             reads=[r_vsq, r_st], writes=[r_st])
                        S.op("dve", lambda e: e.tensor_scalar(stt[:, 8:12], stt[:, 0:4], 1.0 / P, None, ALU.mult), reads=[r_st], writes=[r_st])
                        S.op("dve", lambda e: e.tensor_tensor(stt[:, 12:16], stt[:, 8:12], stt[:, 8:12], ALU.mult), reads=[r_st], writes=[r_st])
                        S.op("dve", lambda e: e.scalar_tensor_tensor(stt[:, 16:20], stt[:, 4:8], 1.0 / P, stt[:, 12:16], ALU.mult, ALU.subtract),
                             reads=[r_st], writes=[r_st])
                        S.op("act", lambda e: e.activation(stt[:, 20:24], stt[:, 16:20], AF.Sqrt, bias=self.eps_t[:, 0:1]),
                             reads=[r_st, self.r_const], writes=[r_st])
                        S.op("dve", lambda e: e.reciprocal(stt[:, 20:24], stt[:, 20:24]), reads=[r_st], writes=[r_st])
                        ob, rob = vnring.next()
                        for gq in range(4):
                            S.op("dve", lambda e, gq=gq, vb=vb, ob=ob: e.tensor_scalar(
                                ob[:, gq * P:(gq + 1) * P], vb[:, gq * P:(gq + 1) * P], stt[:, 8 + gq:9 + gq], stt[:, 20 + gq:21 + gq],
                                ALU.subtract, ALU.mult), reads=[rvb, r_st], writes=[rob])
                        self.dma("sp", self.vn_d[ch, :, (s - 4) * 512:(s - 3) * 512], ob, f"vno{vnring.i}", reads=[rob])
                else:
                    pb, rpb = pring.next()
                    self.mmgroup(pb[:, :n], [(buf[:, k, 0:P], hT[:, k, t0:t0 + n]) for k in range(DC)], [rb, r_hT], [rpb])
                    if j == 0:
                        S.op("dve", lambda e, pb=pb, t0=t0, n=n: e.tensor_tensor(t1[0:64, :n], pb[64:128, :n], sinT[64:128, t0:t0 + n], ALU.mult),
                             reads=[rpb, r_rope], writes=[r_t1])
                        S.op("dve", lambda e, pb=pb, t0=t0, n=n: e.tensor_tensor(t2[0:64, :n], pb[0:64, :n], cosT[0:64, t0:t0 + n], ALU.mult),
                             reads=[rpb, r_rope], writes=[r_t2])
                        S.op("dve", lambda e, t0=t0, n=n: e.tensor_tensor(self.krT[0:64, t0:t0 + n], t1[0:64, :n], t2[0:64, :n], ALU.add),
                             reads=[r_t1, r_t2], writes=[self.r_krT])
                    else:
                        S.op("act", lambda e, pb=pb, t0=t0, n=n: e.activation(self.krT[0:64, t0:t0 + n], pb[0:64, :n], AF.Copy),
                             reads=[rpb], writes=[self.r_krT])
        for cj in casts:
            cj()
        if "p0" in self.debug:
            self.dump("cqn", self.cqn, [P, 4, NT], BF16, self.r_cqn)
            self.dump("ckvn", self.ckvn, [P, 4, NT], BF16, self.r_ckvn)
            self.dump("krT", self.krT[0:64], [64, NT], BF16, self.r_krT)
        S.end("p0")

    def phase_att0(self):
        S = self.S
        S.begin()
        casts = iter(self.cast_parts("w_in1", 16) + self.cast_parts("w_out1", 8))
        self.attT = self.V(0, [P, 8, NT], BF16)
        self.r_attT = Res("attT")
        o = 36864
        Wq = self.V(o, [P, 4, 2048], BF16); o += 16384
        Wkv = self.V(o, [P, 4, 2048], BF16); o += 16384
        sets = []
        for i in range(2):
            sets.append(dict(QT=self.V(o, [P, NT], BF16), QrT=self.V(o + 4608, [P, NT], BF16),
                             KT=self.V(o + 9216, [P, NT], BF16), Vh=self.V(o + 13824, [P, 18, P], BF16),
                             r=Res(f"hset{i}")))
            o += 18432
        cosT, sinT, r_rope = self.load_rope(o, S); o += 16384
        t1 = self.V(o, [P, 512], F32); o += 2048
        t2 = self.V(o, [P, 512], F32); o += 2048
        PT = [self.V(o + i * 1024, [P, 512], BF16) for i in range(3)]; o += 3072
        rl = [self.V(o + i * 2048, [P, 512], F32) for i in range(2)]; o += 4096
        mslab = [self.V(o + i * 12288, [P, DC, 384], BF16) for i in range(2)]; o += 24576
        assert o <= TOP0, o
        mring = Ring(mslab, "mslabB")
        mpring = Ring([self.bank(7)[:, 0:30]], "pmodB")
        bg = []
        bg += self.mod_jobs(0, 11, 32, mring, mpring)
        bg.append(lambda: self.mod_finish(0, 32, 96))
        bg += self.mod_jobs(1, 0, 32, mring, mpring)
        bg.append(lambda: self.mod_finish(1, 0, 96))
        bg = iter(bg)

        def run_bg(k):
            for _ in range(k):
                j = next(bg, None)
                if j is not None:
                    j()
        r_w, r_t1, r_t2 = Res("w"), Res("t1"), Res("t2")
        self.dma("sp", Wq, self.wb["w_q"].ap().rearrange("(k p) n -> p k n", p=P), "wq", writes=[r_w], ext=self.wext("w_q"))
        self.dma("sp", Wkv, self.wb["w_kv"].ap().rearrange("(k p) n -> p k n", p=P), "wkv", writes=[r_w], ext=self.wext("w_kv"))
        sring = Ring([self.bank(i) for i in range(3)], "S")
        oring = Ring([self.bank(3), self.bank(4)], "O")
        lring = Ring([self.bank(5), self.bank(6)], "L")
        ptring = Ring(PT, "PT")
        rlring = Ring(rl, "rl")
        cqn, ckvn, krT = self.cqn, self.ckvn, self.krT
        for h in range(8):
            hs = sets[h % 2]
            rh = hs["r"]
            for (t0, n, j) in TILES:
                pb, rpb = sring.next()
                self.mmgroup(pb[:, :n], [(Wq[:, k, h * P:(h + 1) * P], cqn[:, k, t0:t0 + n]) for k in range(4)],
                             [r_w, self.r_cqn], [rpb])
                S.op("act", lambda e, pb=pb, hs=hs, t0=t0, n=n: e.activation(hs["QT"][:, t0:t0 + n], pb[:, :n], AF.Copy),
                     reads=[rpb], writes=[rh])
                pb, rpb = sring.next()
                self.mmgroup(pb[:, :n], [(Wq[:, k, 1024 + h * P:1024 + (h + 1) * P], cqn[:, k, t0:t0 + n]) for k in range(4)],
                             [r_w, self.r_cqn], [rpb])
                if j == 0:
                    S.op("dve", lambda e, pb=pb, t0=t0, n=n: e.tensor_tensor(t1[0:64, :n], pb[64:128, :n], sinT[64:128, t0:t0 + n], ALU.mult),
                         reads=[rpb, r_rope], writes=[r_t1])
                    S.op("dve", lambda e, pb=pb, t0=t0, n=n: e.tensor_tensor(t2[0:64, :n], pb[0:64, :n], cosT[0:64, t0:t0 + n], ALU.mult),
                         reads=[rpb, r_rope], writes=[r_t2])
                    S.op("dve", lambda e, hs=hs, t0=t0, n=n: e.tensor_tensor(hs["QrT"][0:64, t0:t0 + n], t1[0:64, :n], t2[0:64, :n], ALU.add),
                         reads=[r_t1, r_t2], writes=[rh])
                else:
                    S.op("act", lambda e, pb=pb, hs=hs, t0=t0, n=n: e.activation(hs["QrT"][0:64, t0:t0 + n], pb[0:64, :n], AF.Copy),
                         reads=[rpb], writes=[rh])
                pb, rpb = sring.next()
                self.mmgroup(pb[:, :n], [(Wkv[:, k, h * P:(h + 1) * P], ckvn[:, k, t0:t0 + n]) for k in range(4)],
                             [r_w, self.r_ckvn], [rpb])
                S.op("dve", lambda e, pb=pb, hs=hs, t0=t0, n=n: e.tensor_copy(hs["KT"][:, t0:t0 + n], pb[:, :n]),
                     reads=[rpb], writes=[rh])
            for cg in range(0, 18, 4):
                nch = min(4, 18 - cg)
                pb, rpb = sring.next()

                def fv(e, pb=pb, cg=cg, nch=nch, h=h):
                    for jj in range(nch):
                        ch = cg + jj
                        for k in range(4):
                            ins = e.matmul(pb[:, jj * P:(jj + 1) * P], ckvn[:, k, ch * P:(ch + 1) * P],
                                           Wkv[:, k, 1024 + h * P:1024 + (h + 1) * P], start=(k == 0), stop=(k == 3))
                    return ins
                S.op("pe", fv, reads=[r_w, self.r_ckvn], writes=[rpb])
                S.op("act", lambda e, pb=pb, hs=hs, cg=cg, nch=nch: e.activation(
                    hs["Vh"][:, cg:cg + nch, :], pb[:, :nch * P].rearrange("p (a b) -> p a b", b=P), AF.Copy),
                    reads=[rpb], writes=[rh])
            for (t0, n, j) in TILES:
                run_bg(2)
                cj = next(casts, None)
                if cj is not None:
                    cj(after=[oring.res[oring.i]] if oring.i >= 0 else [])
                keys = list(range(18)) if j == 0 else [16, 17]
                ob, rob = oring.next()
                lb, rlb = lring.next()

                def s_op(kc):
                    pb, rpb = sring.next()
                    self.mmgroup(pb[:, :n], [(hs["KT"][:, kc * P:(kc + 1) * P], hs["QT"][:, t0:t0 + n]),
                                             (krT[0:64, kc * P:(kc + 1) * P], hs["QrT"][0:64, t0:t0 + n])],
                                 [rh, self.r_krT], [rpb])
                    pt, rpt = ptring.next()
                    S.op("act", lambda e, pb=pb, pt=pt, n=n: e.activation(pt[:, :n], pb[:, :n], AF.Exp, scale=MLA_SCALE),
                         reads=[rpb], writes=[rpt])
                    return pt, rpt
                cur = s_op(keys[0])
                for ki, kc in enumerate(keys):
                    nxt = s_op(keys[ki + 1]) if ki + 1 < len(keys) else None
                    pt, rpt = cur

                    def fpv(e, pt=pt, kc=kc, ki=ki, ob=ob, lb=lb, last=(ki == len(keys) - 1), hs=hs, n=n):
                        e.matmul(ob[:, :n], hs["Vh"][:, kc, :], pt[:, :n], start=(ki == 0), stop=last)
                        return e.matmul(lb[:, :n], self.ones_1[:], pt[:, :n], start=(ki == 0), stop=last)
                    S.op("pe", fpv, reads=[rpt, rh, self.r_const], writes=[rob, rlb])
                    cur = nxt
                rb_, rrl = rlring.next()
                S.op("dve", lambda e, rb_=rb_, lb=lb, n=n: e.reciprocal(rb_[:, :n], lb[:, :n]), reads=[rlb], writes=[rrl])
                S.op("dve", lambda e, rb_=rb_, ob=ob, h=h, t0=t0, n=n: e.tensor_tensor(self.attT[:, h, t0:t0 + n], ob[:, :n], rb_[:, :n], ALU.mult),
                     reads=[rob, rrl], writes=[self.r_attT])
        run_bg(1000)
        for cj in casts:
            cj()
        if "att0" in self.debug:
            self.dump("attT", self.attT, [P, 8, NT], BF16, self.r_attT)
        S.end("att0")

    def phase_g0(self):
        S = self.S
        S.begin()
        self.gatedT = self.V(36864, [P, 8, NT], BF16)
        self.r_gat = Res("gated")
        o = HT_BYTES
        wspf = self.V(o, [P, 8, P], F32); o += 4096
        wsp = self.V(o, [P, 8, P], BF16); o += 2048
        bsp = self.V(o, [P, 1024], F32); o += 4096
        vn = [self.V(o + i * 8192, [P, 4, 1024], BF16) for i in range(2)]; o += 16384
        ut = [self.V(o + i * 8192, [P, 8, 512], BF16) for i in range(2)]; o += 16384
        tmp = [self.V(o + i * 2048, [P, 512], F32) for i in range(2)]; o += 4096
        r_w, r_b = Res("wsp"), Res("bsp")
        self.dma("sp", wspf, self.w_spT, "wspf", writes=[r_w])
        S.op("act", lambda e: e.activation(wsp, wspf, AF.Copy), reads=[r_w], writes=[r_w])
        self.dma("sp", bsp, self.b_sp.partition_broadcast(P)[:, 0, :], "bsp", writes=[r_b])
        vring, uring, tring = Ring(vn, "vn"), Ring(ut, "ut"), Ring(tmp, "tmp")
        pring = Ring([self.bank(i) for i in range(4)], "pg")
        for (t0, n, j) in TILES:
            nch = n // P
            c0 = t0 // P
            vb, rv = vring.next()
            self.dma("sp", vb[:, :nch, :], self.vn_d[c0:c0 + nch].rearrange("c p f -> p c f"), f"vn{vring.i}", writes=[rv])
            ub, ru = uring.next()
            self.dma("sp", ub[:, :, :n], self.u_d[:, :, t0:t0 + n], f"ut{uring.i}", writes=[ru])
            for g in range(8):
                pb, rpb = pring.next()

                def f(e, pb=pb, vb=vb, g=g, nch=nch):
                    for jj in range(nch):
                        ins = e.matmul(pb[:, jj * P:(jj + 1) * P], vb[:, jj, g * P:(g + 1) * P], wsp[:, g, :], start=True, stop=True)
                    return ins
                S.op("pe", f, reads=[rv, r_w], writes=[rpb])
                tb, rt = tring.next()
                bb = bcast_ap(bsp[:, g * P:(g + 1) * P], [[0, nch], [1, P]])
                S.op("dve", lambda e, pb=pb, tb=tb, bb=bb, nch=nch, n=n: e.tensor_tensor(
                    tb[:, :n].rearrange("p (a b) -> p a b", b=P), pb[:, :n].rearrange("p (a b) -> p a b", b=P), bb, ALU.add),
                    reads=[rpb, r_b], writes=[rt])
                S.op("dve", lambda e, tb=tb, ub=ub, g=g, t0=t0, n=n: e.tensor_tensor(
                    self.gatedT[:, g, t0:t0 + n], tb[:, :n], ub[:, g, :n], ALU.mult), reads=[rt, ru], writes=[self.r_gat])
        if "g0" in self.debug:
            self.dump("gatedT", self.gatedT, [P, 8, NT], BF16, self.r_gat)
        S.end("g0")

    def phase_o(self, l):
        S = self.S
        S.begin()
        if l == 0:
            mix = lambda k: self.attT[:, k] if k < 8 else self.gatedT[:, k - 8]
            rmix = [self.r_attT, self.r_gat]
            src, tiles, wname = self.xT, TILES, "w_out0"
        else:
            mix = lambda k: self.attT1[:, k]
            rmix = [self.r_attT1]
            src, tiles, wname = self.xb, TILES[:4], "w_out1"
        srcv = src.rearrange("(c p) t -> p c t", p=P)
        dstv = self.xa.rearrange("(c p) t -> p c t", p=P)
        o = HT_BYTES
        slabs = [self.V(o + i * 16384, [P, DC, 512], BF16) for i in range(2)]; o += 32768
        xp = [self.V(o + i * 8192, [P, 4, 512], F32) for i in range(2)]; o += 16384
        xo = [self.V(o + i * 8192, [P, 4, 512], F32) for i in range(2)]; o += 16384
        ring, xpr, xor_ = Ring(slabs, "slab"), Ring(xp, "xp"), Ring(xo, "xo")
        pring = Ring([self.bank(i) for i in range(4)], "po")
        wv = self.wb[wname].ap().rearrange("(k p) n -> p k n", p=P)
        for s in range(4):
            buf, rb = ring.next()
            self.dma("sp", buf, wv[:, :, s * 512:(s + 1) * 512], f"slab{ring.i}", writes=[rb], ext=self.wext(wname))
            for (t0, n, j) in tiles:
                xi, rxi = xpr.next()
                self.dma("sp", xi[:, :, :n], srcv[:, 4 * s:4 * s + 4, t0:t0 + n], f"xp{xpr.i}", writes=[rxi])
                xq, rxq = xor_.next()
                for mm in range(4):
                    m = 4 * s + mm
                    pb, rpb = pring.next()
                    self.mmgroup(pb[:, :n], [(buf[:, k, mm * P:(mm + 1) * P], mix(k)[:, t0:t0 + n]) for k in range(DC)],
                                 [rb] + rmix, [rpb])
                    g1 = self.modT[:, l, 2 * DC + m, j:j + 1]
                    S.op("dve", lambda e, pb=pb, xi=xi, xq=xq, mm=mm, g1=g1, n=n: e.scalar_tensor_tensor(
                        xq[:, mm, :n], pb[:, :n], g1, xi[:, mm, :n], ALU.mult, ALU.add),
                        reads=[rpb, rxi, self.r_const], writes=[rxq])
                self.dma("sp", dstv[:, 4 * s:4 * s + 4, t0:t0 + n], xq[:, :, :n], f"xo{xor_.i}", reads=[rxq])
        S.end("o")

    def phase_ffn(self, l):
        S = self.S
        S.begin()
        hT, r_hT = self.hT, self.r_hT
        last = (l == 1)
        src = self.xa.rearrange("(c p) t -> p c t", p=P)
        dst = (self.outT if last else self.xb).rearrange("(c p) t -> p c t", p=P)
        if l == 0:
            groups = [(0, [384, 384]), (768, [384, 384]), (1536, [384, 384])]
        else:
            groups = [(0, [384, 384]), (768, [384, 384]), (1536, [512])]
        o = HT_BYTES
        xacc = self.V(o, [P, DC, 768], F32); o += 49152
        w1 = [self.V(o + i * 16384, [P, DC, 512], BF16) for i in range(2)]; o += 32768
        w2 = [self.V(o + i * 16384, [P, 4, 2048], BF16) for i in range(2)]; o += 32768
        aT = [self.V(o + i * 4096, [P, 4, 512], BF16) for i in range(2)]; o += 8192
        rt = [self.V(o + i * 2048, [P, 512], F32) for i in range(2)]; o += 4096
        assert o <= ARENA_BYTES
        r_x, r_sqf = Res("xacc"), Res("sqf")
        w1r, w2r, ar, rr = Ring(w1, "w1"), Ring(w2, "w2"), Ring(aT, "aT"), Ring(rt, "rt")
        uring = Ring([self.bank(i) for i in range(3)], "U")
        yring = Ring([self.bank(3 + i) for i in range(4)], "Y")
        w1v = self.wb[f"w_ff1_{l}"].ap().rearrange("(k p) n -> p k n", p=P)
        w2v = self.wb[f"w_ff2_{l}"].ap().rearrange("(f p) n -> p f n", p=P)
        e1, e2 = self.wext(f"w_ff1_{l}"), self.wext(f"w_ff2_{l}")
        steps = [(gi, fs) for gi in range(len(groups)) for fs in range(16)]

        def wload(fs):
            b1, r1 = w1r.next()
            self.dma("sp", b1, w1v[:, :, fs * 512:(fs + 1) * 512], f"w1{w1r.i}", writes=[r1], ext=e1)
            b2, r2 = w2r.next()
            self.dma("sp", b2, w2v[:, fs * 4:(fs + 1) * 4, :], f"w2{w2r.i}", writes=[r2], ext=e2)
            return b1, r1, b2, r2
        nxt = wload(0)
        fcasts = iter(self.cast_parts("w_ff1_1", 32) + self.cast_parts("w_ff2_1", 32)) if l == 0 else iter(())
        for si, (gi, fs) in enumerate(steps):
            g0, subs = groups[gi]
            ng = sum(subs)
            if fs == 0:
                self.dma("sp", xacc[:, :, :ng], src[:, :, g0:g0 + ng], "xacc", writes=[r_x])
            b1, r1, b2, r2 = nxt
            nxt = wload(steps[si + 1][1]) if si + 1 < len(steps) else None
            if l == 0 and si >= 2:
                for _ in range(2):
                    cj = next(fcasts, None)
                    if cj is not None:
                        cj(after=[r_x])
            so = 0
            for n in subs:
                t0 = g0 + so
                ab, rab = ar.next()
                for fc in range(4):
                    pb, rpb = uring.next()
                    self.mmgroup(pb[:, :n], [(b1[:, k, fc * P:(fc + 1) * P], hT[:, k, t0:t0 + n]) for k in range(DC)],
                                 [r1, r_hT], [rpb])
                    rb_, rrb = rr.next()
                    S.op("act", lambda e, pb=pb, rb_=rb_, n=n: e.activation(rb_[:, :n], pb[:, :n], AF.Relu), reads=[rpb], writes=[rrb])
                    S.op("dve", lambda e, rb_=rb_, ab=ab, fc=fc, n=n: e.tensor_tensor(ab[:, fc, :n], rb_[:, :n], rb_[:, :n], ALU.mult),
                         reads=[rrb], writes=[rab])
                for m in range(DC):
                    pb, rpb = yring.next()
                    self.mmgroup(pb[:, :n], [(b2[:, fc, m * P:(m + 1) * P], ab[:, fc, :n]) for fc in range(4)], [r2, rab], [rpb])
                    segs = []
                    if t0 + n <= NL:
                        segs.append((0, n, 0))
                    elif t0 >= NL:
                        segs.append((0, n, 1))
                    else:
                        segs.append((0, NL - t0, 0))
                        segs.append((NL - t0, n, 1))
                    for (a, b, j) in segs:
                        g2 = self.modT[:, l, 5 * DC + m, j:j + 1]
                        S.op("dve", lambda e, pb=pb, m=m, a=a, b=b, g2=g2, so=so: e.scalar_tensor_tensor(
                            xacc[:, m, so + a:so + b], pb[:, a:b], g2, xacc[:, m, so + a:so + b], ALU.mult, ALU.add),
                            reads=[rpb, self.r_const], writes=[r_x])
                so += n
            if fs == 15:
                if last:
                    for so in range(0, ng, 256):
                        n = 256
                        sqv = hT[:, :, NL:NL + n]
                        S.op("act", lambda e, so=so, n=n, sqv=sqv: e.activation(sqv, xacc[:, :, so:so + n], AF.Square),
                             reads=[r_x], writes=[r_sqf])
                        sd, rsd = rr.next()
                        pb, rpb = uring.next()
                        self.rms_core(sqv, r_sqf, n, DC, self.ones_d, pb, rpb, sd, rsd)
                        for c in range(DC):
                            S.op("dve", lambda e, c=c, so=so, n=n, sd=sd: e.scalar_tensor_tensor(
                                xacc[:, c, so:so + n], xacc[:, c, so:so + n], self.fng_s[:, c:c + 1], sd[:, :n], ALU.mult, ALU.mult),
                                reads=[rsd, self.r_const], writes=[r_x])
                self.dma("sp", dst[:, :, g0:g0 + ng], xacc[:, :, :ng], "xst", reads=[r_x])
        S.end("ffn")

    def phase_p1(self):
        S = self.S
        S.begin()
        hT, r_hT = self.hT, self.r_hT
        self.KTd = self.V(TOP1, [P, 4, NT], BF16)
        self.V1a = self.V(TOP1 + 18432, [P, 18, 4, P], BF16)
        self.r_KTd, self.r_V1a = Res("KTd"), Res("V1a")
        o = HT_BYTES
        slabs = [self.V(o + i * 16384, [P, DC, 512], BF16) for i in range(4)]; o += 65536
        cosT, sinT, r_rope = self.load_rope(o, S); o += 16384
        t1 = [self.V(o + i * 2048, [P, 512], F32) for i in range(2)]; o += 4096
        qo = [self.V(o + i * 1024, [P, 512], BF16) for i in range(2)]; o += 2048
        assert o <= TOP1
        ring = Ring(slabs, "slab")
        tring, qring = Ring(t1, "t1"), Ring(qo, "qo")
        pring = Ring([self.bank(i) for i in range(6)], "pp1")
        wv = self.wb["w_in1"].ap().rearrange("(k p) n -> p k n", p=P)
        ext = self.wext("w_in1")
        S.op("dve", lambda e: e.memset(self.V1a[:, :, :, 64:128], 1.0), writes=[self.r_V1a])
        for s in range(5):
            c_main = s * 512 if s < 4 else 4096
            c_swap = 2048 + s * 512 if s < 4 else 4608
            A, rA = ring.next()
            self.dma("sp", A, wv[:, :, c_main:c_main + 512], f"slab{ring.i}", writes=[rA], ext=ext)
            B, rB = ring.next()
            self.dma("sp", B, wv[:, :, c_swap:c_swap + 512], f"slab{ring.i}", writes=[rB], ext=ext)
            for (t0, n, j) in (TILES[:4] if s < 4 else TILES):
                for mm in range(4):
                    pm, rpm = pring.next()
                    self.mmgroup(pm[:, :n], [(A[:, k, mm * P:(mm + 1) * P], hT[:, k, t0:t0 + n]) for k in range(DC)], [rA, r_hT], [rpm])
                    if j == 1:
                        S.op("act", lambda e, pm=pm, mm=mm, t0=t0, n=n: e.activation(self.KTd[:, mm, t0:t0 + n], pm[:, :n], AF.Copy),
                             reads=[rpm], writes=[self.r_KTd])
                        continue
                    pw, rpw = pring.next()
                    self.mmgroup(pw[:, :n], [(B[:, k, mm * P:(mm + 1) * P], hT[:, k, t0:t0 + n]) for k in range(DC)], [rB, r_hT], [rpw])
                    ta, rta = tring.next()
                    S.op("dve", lambda e, pm=pm, ta=ta, t0=t0, n=n: e.tensor_tensor(ta[:, :n], pm[:, :n], cosT[:, t0:t0 + n], ALU.mult),
                         reads=[rpm, r_rope], writes=[rta])
                    tb, rtb = tring.next()
                    S.op("dve", lambda e, pw=pw, tb=tb, t0=t0, n=n: e.tensor_tensor(tb[:, :n], pw[:, :n], sinT[:, t0:t0 + n], ALU.mult),
                         reads=[rpw, r_rope], writes=[rtb])
                    if s < 4:
                        qb, rqb = qring.next()
                        S.op("dve", lambda e, ta=ta, tb=tb, qb=qb, n=n: e.tensor_tensor(qb[:, :n], ta[:, :n], tb[:, :n], ALU.add),
                             reads=[rta, rtb], writes=[rqb])
                        self.dma("sp", self.q1_d[:, 4 * s + mm, t0:t0 + n], qb[:, :n], f"qo{qring.i}", reads=[rqb])
                    else:
                        S.op("dve", lambda e, ta=ta, tb=tb, mm=mm, t0=t0, n=n: e.tensor_tensor(self.KTd[:, mm, t0:t0 + n], ta[:, :n], tb[:, :n], ALU.add),
                             reads=[rta, rtb], writes=[self.r_KTd])
        A, rA = ring.next()
        self.dma("sp", A[:, :, :256], wv[:, :, 5120:5376], f"slab{ring.i}", writes=[rA], ext=ext)
        for ch in range(18):
            pm, rpm = pring.next()
            self.mmgroup(pm[:, :256], [(hT[:, k, ch * P:(ch + 1) * P], A[:, k, :256]) for k in range(DC)], [rA, r_hT], [rpm])
            S.op("act", lambda e, pm=pm, ch=ch: e.activation(self.V1a[:, ch, :, 0:64], pm[:, :256].rearrange("p (g d) -> p g d", g=4), AF.Copy),
                 reads=[rpm], writes=[self.r_V1a])
        if "p1" in self.debug:
            self.dump("KTd", self.KTd, [P, 4, NT], BF16, self.r_KTd)
            self.dump("V1a", self.V1a, [P, 18, 4, P], BF16, self.r_V1a)
        S.end("p1")

    def phase_att1(self):
        S = self.S
        S.begin()
        self.attT1 = self.V(0, [P, DC, NL], BF16)
        self.r_attT1 = Res("attT1")
        o = 65536
        mkf = self.V(o, [P, 2, 512], F32); o += 4096
        mk = self.V(o, [P, 2, 512], BF16); o += 2048
        idf = self.V(o, [P, P], F32); o += 512
        idb = self.V(o, [P, P], BF16); o += 256
        es = self.V(o, [P, 32], F32); o += 128
        qb = [self.V(o + i * 4096, [P, DC, P], BF16) for i in range(2)]; o += 8192
        PT = [self.V(o + i * 1024, [P, 512], BF16) for i in range(4)]; o += 4096
        rl = [self.V(o + i * 2048, [P, 512], F32) for i in range(2)]; o += 4096
        r_c = Res("c1")
        self.dma("sp", mkf, self.masks, "mkf", writes=[r_c])
        self.dma("sp", idf, self.ident, "idf", writes=[r_c])
        self.dma("sp", es, self.sinks.partition_broadcast(P)[:, 0, :], "es", writes=[r_c])
        S.op("act", lambda e: e.activation(mk, mkf, AF.Copy), reads=[r_c], writes=[r_c])
        S.op("act", lambda e: e.activation(idb, idf, AF.Copy), reads=[r_c], writes=[r_c])
        S.op("act", lambda e: e.activation(es, es, AF.Exp), reads=[r_c], writes=[r_c])
        qring, ptring, rlring = Ring(qb, "qb"), Ring(PT, "PT"), Ring(rl, "rl")
        sring = Ring([self.bank(i) for i in range(4)], "S1")
        oring = Ring([self.bank(4 + i) for i in range(3)], "O1")
        KTd, V1a = self.KTd, self.V1a
        for i in range(16):
            q, rq = qring.next()
            self.dma("sp", q, self.q1_d[:, :, i * P:(i + 1) * P], f"qb{qring.i}", writes=[rq])
            keys = []
            for jj in (i - 1, i, i + 1):
                if 0 <= jj < 16:
                    keys.append((jj, None if jj == i else (0 if jj == i - 1 else 1)))
            keys += [(16, None), (17, None)]
            for g in range(4):
                for half in range(2):
                    lo, hi = half * 64, half * 64 + 64
                    ob, rob = oring.next()

                    def s_op(kc, mid):
                        pb, rpb = sring.next()

                        def f(e, pb=pb, kc=kc, mid=mid, lo=lo, hi=hi, g=g, q=q):
                            ins = e.matmul(pb[:, :].rearrange("p (a b) -> p a b", b=P), KTd[lo:hi, g, kc * P:(kc + 1) * P],
                                           q[lo:hi, 4 * g:4 * g + 4, :], start=True, stop=(mid is None))
                            if mid is not None:
                                ins = e.matmul(pb[:, :], idb, mk[:, mid, :], start=False, stop=True)
                            return ins
                        S.op("pe", f, reads=[self.r_KTd, rq, r_c], writes=[rpb])
                        pt, rpt = ptring.next()
                        S.op("act", lambda e, pb=pb, pt=pt: e.activation(pt, pb, AF.Exp, scale=SWA_SCALE), reads=[rpb], writes=[rpt])
                        return pt, rpt
                    cur = s_op(*keys[0])
                    for ki, (kc, mid) in enumerate(keys):
                        nxt = s_op(*keys[ki + 1]) if ki + 1 < len(keys) else None
                        pt, rpt = cur
                        S.op("pe", lambda e, pt=pt, kc=kc, ki=ki, ob=ob, lastk=(ki == len(keys) - 1), g=g: e.matmul(
                            ob, V1a[:, kc, g, :], pt, start=(ki == 0), stop=lastk), reads=[rpt, self.r_V1a], writes=[rob])
                        cur = nxt
                    rb_, rrl = rlring.next()
                    h0 = 8 * g + half
                    esb = bcast_ap(es[64:128, h0:h0 + 1], [[2, 4], [0, P]])
                    S.op("dve", lambda e, rb_=rb_, ob=ob, esb=esb: e.tensor_tensor(
                        rb_[64:128, :].rearrange("p (a b) -> p a b", b=P), ob[64:128, :].rearrange("p (a b) -> p a b", b=P), esb, ALU.add),
                        reads=[rob, r_c], writes=[rrl])
                    S.op("dve", lambda e, rb_=rb_: e.reciprocal(rb_[64:128, :], rb_[64:128, :]), reads=[rrl], writes=[rrl])
                    S.op("dve", lambda e, rb_=rb_, ob=ob, i=i, lo=lo, hi=hi, g=g: e.tensor_tensor(
                        self.attT1[lo:hi, 4 * g:4 * g + 4, i * P:(i + 1) * P], ob[0:64, :].rearrange("p (a b) -> p a b", b=P),
                        rb_[64:128, :].rearrange("p (a b) -> p a b", b=P), ALU.mult), reads=[rob, rrl], writes=[self.r_attT1])
        if "att1" in self.debug:
            self.dump("attT1", self.attT1, [P, DC, NL], BF16, self.r_attT1)
        S.end("att1")


def _pc(v):
    return np.ascontiguousarray(v.reshape(-1, P).T)


def rope_tables():
    t = np.arange(NL)
    row = (t // 64).astype(np.float32)
    col = (t % 64).astype(np.float32)
    inv = (10000.0 ** (-np.arange(16, dtype=np.float32) / 16)).astype(np.float32)
    ang = np.concatenate([row[:, None] * inv, col[:, None] * inv], axis=-1)
    cos = np.cos(ang).astype(np.float32).T
    sin = np.sin(ang).astype(np.float32).T
    c4 = np.concatenate([cos, cos, cos, cos], 0)
    s4 = np.concatenate([-sin, sin, -sin, sin], 0)
    return np.ascontiguousarray(c4), np.ascontiguousarray(s4)


def swap_halves(w, hd):
    sh = w.shape
    w = w.reshape(sh[:-1] + (sh[-1] // hd, 2, hd // 2))
    return np.ascontiguousarray(w[..., ::-1, :]).reshape(sh)


def make_shared(inp):
    f = np.float32
    sh = {}
    sh["n1g"] = np.ascontiguousarray(np.stack([_pc(inp["norm1_g"][l]) for l in range(2)], 1), f)
    sh["n2g"] = np.ascontiguousarray(np.stack([_pc(inp["norm2_g"][l]) for l in range(2)], 1), f)
    sh["fng"] = _pc(inp["final_norm_g"]).astype(f)
    sh["bmod"] = np.ascontiguousarray(np.stack([_pc(inp["b_mod"][l]) for l in range(2)], 1), f)
    call = np.concatenate([inp["c"], inp["c_ctx"][None], np.zeros((1, D), f)], 0)
    sh["cT"] = np.ascontiguousarray(call.reshape(10, DC, P).transpose(2, 1, 0), f)
    sh["qng"] = _pc(inp["mla_q_norm_g"][0]).astype(f)
    sh["kvng"] = _pc(inp["mla_kv_norm_g"][0]).astype(f)
    sh["w_spT"] = np.ascontiguousarray(inp["gmlp_w_sp"][0].transpose(2, 0, 1), f)
    sh["b_sp"] = np.ascontiguousarray(inp["gmlp_b_sp"][0].reshape(1, 1024), f)
    sh["sinks"] = np.ascontiguousarray(inp["swa_sinks"].reshape(1, 32), f)
    c4, s4 = rope_tables()
    sh["ropec"], sh["ropes"] = c4, s4
    kl = np.arange(P)[:, None]
    ql = np.arange(P)[None, :]
    m_prev = np.where(kl >= ql, 0.0, NEG)
    m_next = np.where(kl <= ql, 0.0, NEG)
    sh["masks"] = np.ascontiguousarray(np.stack([np.tile(m_prev, (1, 4)), np.tile(m_next, (1, 4))], 1), f)
    sh["ident"] = np.eye(P, dtype=f)
    return sh


def make_weights(inp):
    f = np.float32
    W = {}
    wi = inp["even_w_in"][0]
    cq, ckv, kr, gm = wi[:, :512], wi[:, 512:1024], wi[:, 1024:1088], wi[:, 1088:]
    W["w_in0"] = np.concatenate([cq, ckv, gm, kr, swap_halves(kr, 64)], 1)
    wq = inp["mla_w_uq"][0].reshape(512, 8, 192)
    qn = wq[:, :, :128].reshape(512, 1024)
    qr = wq[:, :, 128:]
    W["w_q"] = np.concatenate([qn, np.concatenate([qr, swap_halves(qr, 64)], 2).reshape(512, 1024)], 1)
    wkv = inp["mla_w_ukv"][0].reshape(512, 8, 256)
    W["w_kv"] = np.concatenate([wkv[:, :, :128].reshape(512, 1024), wkv[:, :, 128:].reshape(512, 1024)], 1)
    W["w_out0"] = inp["even_w_out"][0]
    w1 = inp["odd_w_in"][0]
    q, k, v = w1[:, :2048], w1[:, 2048:2304], w1[:, 2304:]
    kd = np.repeat(k.reshape(D, 4, 1, 64), 2, axis=2).reshape(D, 512)
    W["w_in1"] = np.concatenate([q, swap_halves(q, 64), kd, swap_halves(kd, 64), v], 1)
    W["w_out1"] = inp["odd_w_out"][0]
    for l in range(2):
        W[f"w_ff1_{l}"] = inp["w_ff1"][l]
        W[f"w_ff2_{l}"] = inp["w_ff2"][l]
    return {k: np.ascontiguousarray(v, f) for k, v in W.items()}


def make_core_inputs(inp, b, r, nr, shared, W):
    m = dict(shared)
    m["xT"] = np.ascontiguousarray(np.concatenate([inp["x"][b], inp["ctx"][b]], 0).T)
    sel = np.zeros((P, 8), np.float32)
    sel[:, b] = 1.0
    m["sel"] = sel
    cpr = 6 * D // nr
    m["w_mod"] = np.ascontiguousarray(inp["w_mod"][:, :, r * cpr:(r + 1) * cpr])
    for (name, rows, cols) in WSPEC:
        rp = rows // nr
        m["ws_" + name] = np.ascontiguousarray(W[name][r * rp:(r + 1) * rp])
    return m


_PROG = None
NSHARD = 1


def kernel(**inputs):
    global _PROG
    inp = {k: np.asarray(v) for k, v in inputs.items()}
    if _PROG is None:
        _PROG = Prog(ncores=NSHARD)
        _PROG.build()
    shared = make_shared(inp)
    W = make_weights(inp)
    if NSHARD == 1:
        in_maps = [make_core_inputs(inp, b, 0, 1, shared, W) for b in range(8)]
    else:
        in_maps = [make_core_inputs(inp, b, b, 8, shared, W) for b in range(8)]
    res = run_bass_kernel_spmd(_PROG.nc, in_maps, core_ids=list(range(8)))
    out = np.stack([np.ascontiguousarray(r["outT"].T) for r in res.results], 0)
    return out.astype(np.float32)
```
